# Optimizing a Trainium2 kernel written in Bass

```python
import jax, jax.numpy as jnp
from jax import lax
import numpy as np

D_MODEL = 1024
BATCH = 4
SEQ = 4096
DEPTH = 2

CHUNK = 64
N_A_LAYERS = DEPTH // 2
N_B_LAYERS = DEPTH - N_A_LAYERS
MLA_HEADS = 8
Q_LORA = 384
KV_LORA = 256
NOPE_DIM = 128
ROPE_DIM = 64
V_DIM = 128
ROPE_THETA = 10000.0
Q_BLOCK = 128
B_HEADS = 16
B_HEAD_DIM = 64
LEFT_CHUNKS = 8
BAND = (LEFT_CHUNKS + 1) * CHUNK
MAX_REL = 256
REL_TABLE = MAX_REL + CHUNK
D_FF = -(-8 * D_MODEL // (3 * 256)) * 256
EPS = 1e-6

kernel_name = "yoco_mla_chunked_relbias_swiglu_sandwich"


def rms_norm(x, g):
    xf = x.astype(jnp.float32)
    y = xf * lax.rsqrt(jnp.mean(xf * xf, axis=-1, keepdims=True) + EPS)
    return (y * g.astype(jnp.float32)).astype(x.dtype)


def rope_tables(positions, dtype):
    inv = 1.0 / (ROPE_THETA ** (jnp.arange(0, ROPE_DIM, 2, dtype=jnp.float32) / ROPE_DIM))
    ang = positions.astype(jnp.float32)[..., None] * inv
    return jnp.cos(ang).astype(dtype), jnp.sin(ang).astype(dtype)


def apply_rope(t, cos, sin):
    t1, t2 = jnp.split(t, 2, axis=-1)
    return jnp.concatenate([t1 * cos - t2 * sin, t2 * cos + t1 * sin], axis=-1)


def swiglu(h, w_gate, w_up, w_down):
    return (jax.nn.silu(h @ w_gate) * (h @ w_up)) @ w_down


def mla_mixer(h, cos, sin, w_a, g_q, w_uq, g_kv, w_ukv, w_o):
    B, S, _ = h.shape
    a = h @ w_a
    cq, ckv, kr = jnp.split(a, [Q_LORA, Q_LORA + KV_LORA], axis=-1)
    q = (rms_norm(cq, g_q) @ w_uq).reshape(B, S, MLA_HEADS, NOPE_DIM + ROPE_DIM)
    q_nope = q[..., :NOPE_DIM]
    q_rope = apply_rope(q[..., NOPE_DIM:], cos[:, :, None], sin[:, :, None])
    kv = (rms_norm(ckv, g_kv) @ w_ukv).reshape(B, S, MLA_HEADS, NOPE_DIM + V_DIM)
    k_nope, v = kv[..., :NOPE_DIM], kv[..., NOPE_DIM:]
    k_rope = apply_rope(kr, cos, sin)
    scale = (NOPE_DIM + ROPE_DIM) ** -0.5
    n_blk = S // Q_BLOCK
    key_chunk = jnp.arange(S) // CHUNK

    def to_blocks(t):
        return jnp.moveaxis(t.reshape(B, n_blk, Q_BLOCK, *t.shape[2:]), 1, 0)

    def block_attn(args):
        qn, qr, blk = args
        s = (jnp.einsum('bqhd,bkhd->bhqk', qn, k_nope, preferred_element_type=jnp.float32)
             + jnp.einsum('bqhr,bkr->bhqk', qr, k_rope, preferred_element_type=jnp.float32)) * scale
        q_chunk = (blk * Q_BLOCK + jnp.arange(Q_BLOCK)) // CHUNK
        mask = key_chunk[None, :] <= q_chunk[:, None]
        s = jnp.where(mask[None, None], s, -jnp.inf)
        p = jax.nn.softmax(s, axis=-1).astype(v.dtype)
        return jnp.einsum('bhqk,bkhd->bqhd', p, v)

    o = lax.map(block_attn, (to_blocks(q_nope), to_blocks(q_rope), jnp.arange(n_blk)))
    o = jnp.moveaxis(o, 0, 1).reshape(B, S, MLA_HEADS * V_DIM)
    return o @ w_o


def gather_band(t):
    B, S = t.shape[:2]
    n_c = S // CHUNK
    tp = jnp.pad(t, ((0, 0), (LEFT_CHUNKS * CHUNK, 0), (0, 0), (0, 0)))
    tp = tp.reshape(B, n_c + LEFT_CHUNKS, CHUNK, *t.shape[2:])
    band = jnp.stack([tp[:, j:j + n_c] for j in range(LEFT_CHUNKS + 1)], axis=2)
    return band.reshape(B, n_c, BAND, *t.shape[2:])


def shared_kv(x, g_src, w_kv):
    B, S, _ = x.shape
    kv = (rms_norm(x, g_src) @ w_kv).reshape(B, S, 2, B_HEADS, B_HEAD_DIM)
    return gather_band(kv[:, :, 0]), gather_band(kv[:, :, 1])


def chunked_mixer(h, k_band, v_band, w_q, rel_table, w_o):
    B, S, _ = h.shape
    n_c = S // CHUNK
    q = (h @ w_q).reshape(B, n_c, CHUNK, B_HEADS, B_HEAD_DIM)
    dist = LEFT_CHUNKS * CHUNK + jnp.arange(CHUNK)[:, None] - jnp.arange(BAND)[None, :]
    idx = jnp.clip(dist, -(CHUNK - 1), MAX_REL) + (CHUNK - 1)
    bias = rel_table.astype(jnp.float32)[:, idx]
    valid = (jnp.arange(n_c)[:, None] - LEFT_CHUNKS + jnp.arange(BAND)[None, :] // CHUNK) >= 0
    s = jnp.einsum('bnqhd,bnkhd->bnhqk', q, k_band, preferred_element_type=jnp.float32)
    s = s * (B_HEAD_DIM ** -0.5) + bias[None, None]
    s = jnp.where(valid[None, :, None, None, :], s, -jnp.inf)
    p = jax.nn.softmax(s, axis=-1).astype(v_band.dtype)
    o = jnp.einsum('bnhqk,bnkhd->bnqhd', p, v_band).reshape(B, S, B_HEADS * B_HEAD_DIM)
    return o @ w_o


def setup_inputs(seed: int = 0) -> dict:
    key = jax.random.key(seed)
    ks = iter(jax.random.split(key, 32))
    f32 = jnp.float32

    def w(shape, fan_in):
        return jax.random.normal(next(ks), shape, f32) * (fan_in ** -0.5)

    def gain(shape):
        return 1.0 + 0.05 * jax.random.normal(next(ks), shape, f32)

    x = jax.random.normal(next(ks), (BATCH, SEQ, D_MODEL), f32)
    offsets = jax.random.randint(next(ks), (BATCH, 1), 0, 4096, dtype=jnp.int32)
    positions = (jnp.arange(SEQ, dtype=jnp.int32)[None, :] + offsets).astype(jnp.int32)
    return {
        "x": x,
        "positions": positions,
        "attn_pre_g": gain((DEPTH, D_MODEL)),
        "attn_post_g": gain((DEPTH, D_MODEL)),
        "ffn_pre_g": gain((DEPTH, D_MODEL)),
        "ffn_post_g": gain((DEPTH, D_MODEL)),
        "ffn_w_gate": w((DEPTH, D_MODEL, D_FF), D_MODEL),
        "ffn_w_up": w((DEPTH, D_MODEL, D_FF), D_MODEL),
        "ffn_w_down": w((DEPTH, D_FF, D_MODEL), D_FF),
        "mla_w_a": w((N_A_LAYERS, D_MODEL, Q_LORA + KV_LORA + ROPE_DIM), D_MODEL),
        "mla_g_q": gain((N_A_LAYERS, Q_LORA)),
        "mla_w_uq": w((N_A_LAYERS, Q_LORA, MLA_HEADS * (NOPE_DIM + ROPE_DIM)), Q_LORA),
        "mla_g_kv": gain((N_A_LAYERS, KV_LORA)),
        "mla_w_ukv": w((N_A_LAYERS, KV_LORA, MLA_HEADS * (NOPE_DIM + V_DIM)), KV_LORA),
        "mla_w_o": w((N_A_LAYERS, MLA_HEADS * V_DIM, D_MODEL), MLA_HEADS * V_DIM),
        "kv_src_g": gain((D_MODEL,)),
        "w_kv_shared": w((D_MODEL, 2 * B_HEADS * B_HEAD_DIM), D_MODEL),
        "b_w_q": w((N_B_LAYERS, D_MODEL, B_HEADS * B_HEAD_DIM), D_MODEL),
        "b_rel_table": 0.2 * jax.random.normal(next(ks), (N_B_LAYERS, B_HEADS, REL_TABLE), f32),
        "b_w_o": w((N_B_LAYERS, B_HEADS * B_HEAD_DIM, D_MODEL), B_HEADS * B_HEAD_DIM),
    }


def reference(x, positions, attn_pre_g, attn_post_g, ffn_pre_g, ffn_post_g,
              ffn_w_gate, ffn_w_up, ffn_w_down,
              mla_w_a, mla_g_q, mla_w_uq, mla_g_kv, mla_w_ukv, mla_w_o,
              kv_src_g, w_kv_shared, b_w_q, b_rel_table, b_w_o):
    cos, sin = rope_tables(positions, x.dtype)
    k_band = v_band = None
    for layer in range(DEPTH):
        h = rms_norm(x, attn_pre_g[layer])
        if layer < N_A_LAYERS:
            a = layer
            y = mla_mixer(h, cos, sin, mla_w_a[a], mla_g_q[a], mla_w_uq[a],
                          mla_g_kv[a], mla_w_ukv[a], mla_w_o[a])
        else:
            b = layer - N_A_LAYERS
            y = chunked_mixer(h, k_band, v_band, b_w_q[b], b_rel_table[b], b_w_o[b])
        x = x + rms_norm(y, attn_post_g[layer])
        h = rms_norm(x, ffn_pre_g[layer])
        f = swiglu(h, ffn_w_gate[layer], ffn_w_up[layer], ffn_w_down[layer])
        x = x + rms_norm(f, ffn_post_g[layer])
        if layer == N_A_LAYERS - 1:
            k_band, v_band = shared_kv(x, kv_src_g, w_kv_shared)
    return x
```

```python
import numpy as np
from contextlib import ExitStack
import concourse.bass as bass
import concourse.mybir as mybir
from concourse.bass_utils import run_bass_kernel_spmd

F32 = mybir.dt.float32
BF16 = mybir.dt.bfloat16
I32 = mybir.dt.int32
AF = mybir.ActivationFunctionType
ALU = mybir.AluOpType

D = 1024
NCTX = 32
NOWN = 18
OWN0 = NCTX - NOWN
TOWN = NOWN * 128
DFF = 2816
NF = DFF // 128
EPS = 1e-6
NEG = -30000.0
TWO_PI = float(2 * np.pi * (1 - 1e-6))
MLA_SCALE = float(192 ** -0.5)
ENGS = ("pe", "act", "dve", "pool", "sp")


class Op:
    __slots__ = ("eng", "fn", "deps", "dma", "sig", "sem", "semval", "idx", "prewait")

    def __init__(self, eng, fn, dma):
        self.eng = eng
        self.fn = fn
        self.dma = dma
        self.deps = set()
        self.sig = False
        self.sem = None
        self.semval = 0
        self.prewait = None


class Prog:
    def __init__(self, nc, n_dma_sems=(24, 12)):
        self.nc = nc
        self.ops = []
        self.last_w = {}
        self.readers = {}
        self.n_dma_sems = {"sp": n_dma_sems[0], "pool": n_dma_sems[1]}
        self.dma_since_bar = []
        self.nbar = 0

    def op(self, eng, fn, reads=(), writes=(), dma=False):
        o = Op(eng, fn, dma)
        o.idx = len(self.ops)
        key = ("dma", o.idx) if dma else eng
        for r in reads:
            o.deps.update(self.last_w.get(r, {}).values())
            if self._is_psum(r):
                o.deps.update(i for k, i in self.readers.get(r, {}).items() if k != key)
        for w_ in writes:
            o.deps.update(self.last_w.get(w_, {}).values())
            o.deps.update(self.readers.get(w_, {}).values())
        for r in reads:
            self._put(self.readers.setdefault(r, {}), key, o.idx)
        for w_ in writes:
            self._put(self.last_w.setdefault(w_, {}), key, o.idx)
        o.deps.discard(o.idx)
        self.ops.append(o)
        if dma:
            self.dma_since_bar.append(o.idx)
        return o

    @staticmethod
    def _is_psum(r):
        n = r[0] if isinstance(r, tuple) else r
        return isinstance(n, str) and n.startswith("ps")

    @staticmethod
    def _put(d, key, idx):
        d[key] = idx
        if len(d) > 24:
            for k in sorted((k for k in d if isinstance(k, tuple)), key=lambda k: k[1])[:8]:
                del d[k]

    def barrier(self):
        n = self.nbar
        self.nbar += 1
        sig = []
        for e in ("pe", "act", "dve", "pool"):
            sig.append(self.op(e, lambda eng: eng.drain(), writes=[("bar", n, e)]).idx)
        extra = set(sig) | set(self.dma_since_bar)
        self.dma_since_bar = []
        for e in ENGS:
            o = self.op(e, lambda eng: None)
            o.deps |= extra

    def emit(self, stack):
        nc = self.nc
        ops = self.ops
        for o in ops:
            if o.eng == "pe" and not o.dma:
                o.deps = {d for d in o.deps if not (ops[d].eng == "pe" and not ops[d].dma)}
        for o in ops:
            for d in o.deps:
                ops[d].sig = True
        sems = {e: stack.enter_context(nc.semaphore("c_" + e)) for e in ("pe", "act", "dve", "pool")}
        dsems = {q: [stack.enter_context(nc.semaphore("d_%s%d" % (q, i))) for i in range(n)]
                 for q, n in self.n_dma_sems.items()}
        cnt = {e: 0 for e in ENGS}
        dcnt = {q: 0 for q in dsems}
        duse = {q: [0] * len(dsems[q]) for q in dsems}
        for o in ops:
            if o.dma:
                q = o.eng
                i = dcnt[q] % len(dsems[q])
                dcnt[q] += 1
                o.sem = dsems[q][i]
                if duse[q][i] > 0:
                    o.prewait = (o.sem, 16 * duse[q][i])
                duse[q][i] += 1
                o.semval = 16 * duse[q][i]
            elif o.sig:
                cnt[o.eng] += 1
                o.sem = sems[o.eng]
                o.semval = cnt[o.eng]
        per = {e: [] for e in ENGS}
        for o in ops:
            per[o.eng].append(o)

        def run(ename, eng):
            waited = {}
            for o in per[ename]:
                need = {}
                if o.prewait is not None:
                    need[id(o.prewait[0])] = o.prewait
                for d in o.deps:
                    p = ops[d]
                    k = id(p.sem)
                    if k not in need or need[k][1] < p.semval:
                        need[k] = (p.sem, p.semval)
                for k, (s, v) in need.items():
                    if waited.get(k, 0) >= v:
                        continue
                    eng.wait_ge(s, v)
                    waited[k] = v
                ins = o.fn(eng)
                if ins is None:
                    continue
                if o.dma:
                    ins.then_inc(o.sem, 16)
                elif o.sig:
                    ins.then_inc(o.sem, 1)

        block = stack.enter_context(nc.Block())

        @block.tensor
        def _(e):
            run("pe", e)

        @block.scalar
        def _(e):
            run("act", e)

        @block.vector
        def _(e):
            run("dve", e)

        @block.gpsimd
        def _(e):
            run("pool", e)

        @block.sync
        def _(e):
            run("sp", e)


C_GPRE = 0
C_GQ = 40
C_GKV = 43
C_INV = 45
C_IDENT = 77
NCONST = C_IDENT + 128

STOP_PHASES = ("C", "D", "F", "G")


def build(stop="G"):
    nc = bass.Bass("TRN2", target_bir_lowering=False)

    def din(name, shape, dt=F32):
        return nc.dram_tensor(name, list(shape), dt, kind="ExternalInput").ap()

    xin = din("xin", [NCTX * 128, D])
    posT = din("posT", [128, NCTX], I32)
    mb_d = din("mb", [128, NCTX])
    consts_d = din("consts", [128, NCONST])
    gpost_d = din("gpost", [4, D])
    w_a_kv_d = din("w_a_kv", [128, 8, 384])
    w_a_q_d = din("w_a_q", [128, 8, 384])
    w_uq_n_d = din("w_uq_n", [128, 3, 1024])
    w_uq_r_d = din("w_uq_r", [128, 3, 512])
    w_uq_rs_d = din("w_uq_rs", [128, 3, 512])
    w_ukv_k_d = din("w_ukv_k", [128, 2, 1024])
    w_ukv_v_d = din("w_ukv_v", [128, 2, 1024])
    w_o_d = din("w_o", [128, 8, 1024])
    wgu_d = din("wgu", [2, NF, 128, 2, 8, 128])
    wd_d = din("wd", [2, 128, NF, 1024])
    w_kvs_k_d = din("w_kvs_k", [128, 8, 1024])
    w_kvs_v_d = din("w_kvs_v", [128, 8, 1024])
    w_q2_d = din("w_q2", [128, 8, 1024])
    w_o2_d = din("w_o2", [128, 8, 1024])
    btab_d = din("btab", [8, 128, 5, 2, 128])
    out_d = nc.dram_tensor("out", [TOWN, D], F32, kind="ExternalOutput").ap()
    wgu_bf = nc.dram_tensor("wgu_bf", [2, NF, 128, 2, 8, 128], BF16).ap()
    wd_bf = nc.dram_tensor("wd_bf", [2, 128, NF, 1024], BF16).ap()

    P = Prog(nc)
    uid = [0]

    def nm(s):
        uid[0] += 1
        return "%s_%d" % (s, uid[0])

    with ExitStack() as top:
        def sbuf(st, name, shape, dt, side=None):
            return st.enter_context(nc.sbuf_tensor(nm(name), list(shape), dt, side=side))

        def psum(st, name, shape, dt=F32):
            return st.enter_context(nc.psum_tensor(nm(name), list(shape), dt))

        consts = sbuf(top, "consts", [128, NCONST], F32)
        identb = sbuf(top, "identb", [128, 128], BF16)
        onesb = sbuf(top, "onesb", [128, 128], BF16)
        mbt = sbuf(top, "mbt", [128, NCTX], F32)
        mhalf = sbuf(top, "mhalf", [128, 1], F32)
        stat = sbuf(top, "stat", [128, 64], F32)
        gT = sbuf(top, "gT", [128, D], F32)
        tmpA = sbuf(top, "tmpA", [128, D], F32)
        junk = sbuf(top, "junk", [128, D], BF16)
        xs_ring = [sbuf(top, "xs", [128, D], BF16) for _ in range(2)]
        gB = sbuf(top, "gB", [128, 8, 128], BF16)

        P.op("sp", lambda e: e.dma_start(out=consts[:], in_=consts_d), writes=["consts"], dma=True)
        P.op("sp", lambda e: e.dma_start(out=mbt[:], in_=mb_d), writes=["mbt"], dma=True)
        P.op("dve", lambda e: e.tensor_copy(out=identb[:], in_=consts[:, C_IDENT:C_IDENT + 128]),
             reads=["consts"], writes=["identb"])
        P.op("dve", lambda e: e.memset(onesb[:], 1.0), writes=["onesb"])
        P.op("dve", lambda e: e.memset(mhalf[:], -0.5), writes=["mhalf"])

        stat_i = [0]

        def stat_col():
            i = stat_i[0] % 64
            stat_i[0] += 1
            return stat[:, i:i + 1], ("stat", i)

        def load_gB(col0, nch=8):
            for c in range(nch):
                P.op("dve", lambda e, c=c: e.tensor_scalar(out=gB[:, c, :], in0=onesb[:], scalar1=consts[:, col0 + c:col0 + c + 1],
                                                           scalar2=None, op0=ALU.mult),
                     reads=["onesb", "consts"], writes=["gB"])

        def load_gT(row):
            P.op("sp", lambda e: e.dma_start(out=gT[:], in_=gpost_d[row].partition_broadcast(128)), writes=["gT"], dma=True)

        def rstd_of(src_ap, src_res, width, dim):
            ss, ss_r = stat_col()
            rs, rs_r = stat_col()
            P.op("act", lambda e: e.activation(out=junk[:, 0:width], in_=src_ap, func=AF.Square, accum_out=ss),
                 reads=list(src_res), writes=["junk", ss_r])
            P.op("dve", lambda e: e.tensor_scalar(out=ss, in0=ss, scalar1=1.0 / dim, scalar2=EPS, op0=ALU.mult, op1=ALU.add),
                 reads=[ss_r], writes=[ss_r])
            P.op("pool", lambda e: e.tensor_tensor(out=rs, in0=ss, in1=mhalf[:], op=ALU.pow), reads=[ss_r, "mhalf"], writes=[rs_r])
            return rs, rs_r

        xs_i = [0]

        def norm_apply(src_ap, src_res, rs, rs_r, dst_ap, dst_res, ps_tr, ps_res, nch=8):
            xs = xs_ring[xs_i[0] % 2]
            xs_r = ("xs", xs_i[0] % 2)
            xs_i[0] += 1
            P.op("act", lambda e: e.activation(out=xs[:, 0:nch * 128], in_=src_ap, func=AF.Copy, scale=rs),
                 reads=list(src_res) + [rs_r], writes=[xs_r])
            for c in range(nch):
                P.op("pe", lambda e, c=c: e.transpose(out=ps_tr[:, c, :], in_=xs[:, c * 128:(c + 1) * 128], identity=identb[:]),
                     reads=[xs_r, "identb"], writes=[ps_res])
            P.op("dve", lambda e: e.tensor_tensor(out=dst_ap, in0=ps_tr[:, 0:nch, :], in1=gB[:, 0:nch, :], op=ALU.mult),
                 reads=[ps_res, "gB"], writes=list(dst_res))

        def norm_T(src_ap, src_res, dst_ap, dst_res, ps_tr, ps_res, nch=8):
            rs, rs_r = rstd_of(src_ap, src_res, nch * 128, nch * 128)
            norm_apply(src_ap, src_res, rs, rs_r, dst_ap, dst_res, ps_tr, ps_res, nch)

        def post_norm_residual(ps_y, ps_res, x_dst, x_dst_res, x_src, x_src_res):
            ps_rl = list(ps_res) if isinstance(ps_res, list) else [ps_res]
            rs, rs_r = rstd_of(ps_y, ps_rl, D, D)
            P.op("dve", lambda e: e.scalar_tensor_tensor(out=tmpA[:], in0=ps_y, scalar=rs, in1=gT[:], op0=ALU.mult, op1=ALU.mult),
                 reads=ps_rl + [rs_r, "gT"], writes=["tmpA"])
            P.op("dve", lambda e: e.tensor_tensor(out=x_dst, in0=tmpA[:], in1=x_src, op=ALU.add),
                 reads=["tmpA"] + list(x_src_res), writes=list(x_dst_res))

        def ffn(st_outer, x1, layer, gpre_col, gpost_row, ps_tr, final_out):
            GT = 6
            G = GT * 128
            NG = NOWN // GT
            with ExitStack() as st:
                wd = sbuf(st, "wd", [128, NF, D], BF16)
                actT = sbuf(st, "actT", [128, NF, G], BF16)
                hT = sbuf(st, "hT", [128, 8, G], BF16)
                wgu = [sbuf(st, "wgu", [128, 2, 8, 128], BF16) for _ in range(4)]
                sg = [sbuf(st, "sg", [128, 512], BF16) for _ in range(2)]
                pA = psum(st, "pA", [128, D])
                pB = psum(st, "pB", [128, D])
                pC = psum(st, "pC", [128, D])
                ps_g = [pA[:, 0:512], pB[:, 0:512]]
                ps_u = [pA[:, 512:1024], pB[:, 512:1024]]
                ps_y = [pC, pA]
                ps_y_r = [[("ps_y", 0)], [("ps_g", 0), ("ps_u", 0)]]
                for half in range(2):
                    P.op("sp", lambda e, half=half: e.dma_start(out=wd[:, half * 11:(half + 1) * 11, :],
                                                                 in_=wd_bf[layer, :, half * 11:(half + 1) * 11, :]),
                         reads=[("wd_bf", layer, f2) for f2 in range(NF // 2)], writes=[("wd", half)], dma=True)
                load_gB(gpre_col)
                load_gT(gpost_row)
                kk = [0]
                yk = [0]

                def d1_stats(g0):
                    return [rstd_of(x1[:, g0 + t, :], [("x1", g0 + t)], D, D) for t in range(GT)]

                def d1_apply(g0, t, rr):
                    norm_apply(x1[:, g0 + t, :], [("x1", g0 + t)], rr[0], rr[1], hT[:, :, t * 128:(t + 1) * 128], ["hT"], ps_tr, "ps_tr")

                def d2(g0):
                    for f in range(NF):
                        k = kk[0]
                        w = wgu[k % 4]
                        w_r = ("wgu", k % 4)
                        P.op("sp", lambda e, w=w, f=f: e.dma_start(out=w[:], in_=wgu_bf[layer, f]),
                             reads=[("wgu_bf", layer, f)], writes=[w_r], dma=True)
                        for pi, (c0, c1) in enumerate(((0, 512), (512, G))):
                            j = 2 * k + pi
                            pg, pu = ps_g[j % 2], ps_u[j % 2]
                            pg_r, pu_r = ("ps_g", j % 2), ("ps_u", j % 2)
                            s_, s_r = sg[j % 2], ("sg", j % 2)
                            n = c1 - c0
                            for c in range(8):
                                P.op("pe", lambda e, pg=pg, w=w, c=c, c0=c0, c1=c1, n=n: e.matmul(
                                    pg[:, 0:n], lhsT=w[:, 0, c, :], rhs=hT[:, c, c0:c1], start=(c == 0), stop=(c == 7)),
                                    reads=[w_r, "hT"], writes=[pg_r])
                            for c in range(8):
                                P.op("pe", lambda e, pu=pu, w=w, c=c, c0=c0, c1=c1, n=n: e.matmul(
                                    pu[:, 0:n], lhsT=w[:, 1, c, :], rhs=hT[:, c, c0:c1], start=(c == 0), stop=(c == 7)),
                                    reads=[w_r, "hT"], writes=[pu_r])
                            P.op("act", lambda e, pg=pg, s_=s_, n=n: e.activation(out=s_[:, 0:n], in_=pg[:, 0:n], func=AF.Silu),
                                 reads=[pg_r], writes=[s_r])
                            P.op("dve", lambda e, pu=pu, s_=s_, f=f, c0=c0, c1=c1, n=n: e.tensor_tensor(
                                out=actT[:, f, c0:c1], in0=pu[:, 0:n], in1=s_[:, 0:n], op=ALU.mult),
                                reads=[pu_r, s_r], writes=["actT"])
                        kk[0] += 1

                def d3_tile(g0, t):
                    py = ps_y[yk[0] % 2]
                    py_r = ps_y_r[yk[0] % 2]
                    yk[0] += 1
                    for hf in range(2):
                        for f in range(NF):
                            P.op("pe", lambda e, py=py, t=t, hf=hf, f=f: e.matmul(
                                py[:, hf * 512:(hf + 1) * 512], lhsT=actT[:, f, t * 128:(t + 1) * 128],
                                rhs=wd[:, f, hf * 512:(hf + 1) * 512], start=(f == 0), stop=(f == NF - 1)),
                                reads=["actT", ("wd", f // 11)], writes=py_r)
                    tt = g0 + t
                    post_norm_residual(py[:], py_r, x1[:, tt, :], [("x1", tt)], x1[:, tt, :], [("x1", tt)])
                    if final_out:
                        P.op("sp", lambda e, tt=tt: e.dma_start(out=out_d[tt * 128:(tt + 1) * 128, :], in_=x1[:, tt, :]),
                             reads=[("x1", tt)], writes=[("out", tt)], dma=True)

                rr = d1_stats(0)
                for t in range(GT):
                    d1_apply(0, t, rr[t])
                for gi_ in range(NG):
                    g0 = gi_ * GT
                    d2(g0)
                    if gi_ + 1 < NG:
                        rr = d1_stats(g0 + GT)
                    for t in range(GT):
                        d3_tile(g0, t)
                        if gi_ + 1 < NG:
                            d1_apply(g0 + GT, t, rr[t])
            P.barrier()

        with ExitStack() as mla:
            oT = sbuf(mla, "oT", [128, 8, TOWN], BF16, side="right")
            with ExitStack() as mla_ab:
                ckvnT = sbuf(mla_ab, "ckvnT", [128, 2, NCTX * 128], BF16, side="right")
                krT = sbuf(mla_ab, "krT", [128, 2, NCTX * 128], BF16, side="right")
                cqnT = sbuf(mla_ab, "cqnT", [128, 3, TOWN], BF16, side="right")
                qrT = sbuf(mla_ab, "qrT", [128, 4, TOWN], BF16, side="right")
                with ExitStack() as pa:
                    cos2 = sbuf(pa, "cos2", [128, NCTX, 64], F32)
                    ssgn = sbuf(pa, "ssgn", [128, NCTX, 64], F32)
                    w_a_kv = sbuf(pa, "w_a_kv", [128, 8, 384], BF16)
                    w_a_q = sbuf(pa, "w_a_q", [128, 8, 384], BF16)
                    w_uq_r = sbuf(pa, "w_uq_r", [128, 3, 512], BF16)
                    w_uq_rs = sbuf(pa, "w_uq_rs", [128, 3, 512], BF16)
                    xt = [sbuf(pa, "xt", [128, D], F32) for _ in range(3)]
                    hTt = [sbuf(pa, "hTt", [128, 8, 128], BF16) for _ in range(2)]
                    akv_s = [sbuf(pa, "akv_s", [128, 512], BF16) for _ in range(2)]
                    aq_s = [sbuf(pa, "aq_s", [128, 384], BF16) for _ in range(2)]
                    gBkv = sbuf(pa, "gBkv", [128, 2, 128], BF16)
                    gBq = sbuf(pa, "gBq", [128, 3, 128], BF16)
                    rtmp = [sbuf(pa, "rtmp", [128, 512], F32) for _ in range(2)]
                    qrr = sbuf(pa, "qrr", [128, 512], BF16)
                    ps_tr = psum(pa, "ps_tr", [128, 8, 128], BF16)
                    ps_akv = [psum(pa, "ps_akv", [128, 512]) for _ in range(2)]
                    ps_aq = psum(pa, "ps_aq", [128, 512])
                    ps_t2 = psum(pa, "ps_t2", [128, 8, 128], BF16)
                    ps_qr = psum(pa, "ps_qr", [128, 512])
                    ps_qrs = psum(pa, "ps_qrs", [128, 512])
                    ps_t3 = psum(pa, "ps_t3", [128, 8, 128], BF16)

                    for dst, src, r in ((w_a_kv, w_a_kv_d, "w_a_kv"), (w_a_q, w_a_q_d, "w_a_q"),
                                        (w_uq_r, w_uq_r_d, "w_uq_r"), (w_uq_rs, w_uq_rs_d, "w_uq_rs")):
                        P.op("pool", lambda e, dst=dst, src=src: e.dma_start(out=dst[:], in_=src), writes=[r], dma=True)
                    load_gB(C_GPRE + 0)
                    for i_ in range(2):
                        P.op("dve", lambda e, i_=i_: e.memset(akv_s[i_][:, 320:448], 0.0), writes=[("akv_s", i_)])
                    for c in range(2):
                        P.op("dve", lambda e, c=c: e.tensor_scalar(out=gBkv[:, c, :], in0=onesb[:], scalar1=consts[:, C_GKV + c:C_GKV + c + 1],
                                                                   scalar2=None, op0=ALU.mult), reads=["onesb", "consts"], writes=["gBkv"])
                    for c in range(3):
                        P.op("dve", lambda e, c=c: e.tensor_scalar(out=gBq[:, c, :], in0=onesb[:], scalar1=consts[:, C_GQ + c:C_GQ + c + 1],
                                                                   scalar2=None, op0=ALU.mult), reads=["onesb", "consts"], writes=["gBq"])
                    with ExitStack() as rp:
                        posi = sbuf(rp, "posi", [128, NCTX], I32)
                        posf = sbuf(rp, "posf", [128, NCTX], F32)
                        u = sbuf(rp, "u", [128, NCTX, 32], F32)
                        ki = sbuf(rp, "ki", [128, NCTX, 32], I32)
                        kf = sbuf(rp, "kf", [128, NCTX, 32], F32)
                        fw = sbuf(rp, "fw", [128, NCTX, 32], F32)
                        P.op("sp", lambda e: e.dma_start(out=posi[:], in_=posT), writes=["posi"], dma=True)
                        P.op("dve", lambda e: e.tensor_copy(out=posf[:], in_=posi[:]), reads=["posi"], writes=["posf"])
                        P.op("dve", lambda e: e.tensor_tensor(out=u[:], in0=posf[:].unsqueeze(2).to_broadcast([128, NCTX, 32]),
                                                              in1=consts[:, C_INV:C_INV + 32].unsqueeze(1).to_broadcast([128, NCTX, 32]),
                                                              op=ALU.mult), reads=["posf", "consts"], writes=["u"])
                        P.op("dve", lambda e: e.tensor_copy(out=ki[:], in_=u[:]), reads=["u"], writes=["ki"])
                        P.op("dve", lambda e: e.tensor_copy(out=kf[:], in_=ki[:]), reads=["ki"], writes=["kf"])
                        P.op("dve", lambda e: e.tensor_tensor(out=u[:], in0=u[:], in1=kf[:], op=ALU.subtract), reads=["u", "kf"], writes=["u"])

                        def wrapped_sin(shift, scale, dst, dst_r):
                            P.op("dve", lambda e: e.tensor_scalar(out=fw[:], in0=u[:], scalar1=shift, scalar2=None, op0=ALU.add), reads=["u"], writes=["fw"])
                            P.op("dve", lambda e: e.tensor_scalar(out=kf[:], in0=fw[:], scalar1=0.5, scalar2=None, op0=ALU.is_gt), reads=["fw"], writes=["kf"])
                            P.op("dve", lambda e: e.tensor_tensor(out=fw[:], in0=fw[:], in1=kf[:], op=ALU.subtract), reads=["fw", "kf"], writes=["fw"])
                            P.op("dve", lambda e: e.tensor_scalar(out=kf[:], in0=fw[:], scalar1=-0.5, scalar2=None, op0=ALU.is_lt), reads=["fw"], writes=["kf"])
                            P.op("dve", lambda e: e.tensor_tensor(out=fw[:], in0=fw[:], in1=kf[:], op=ALU.add), reads=["fw", "kf"], writes=["fw"])
                            P.op("act", lambda e: e.activation(out=dst, in_=fw[:], func=AF.Sin, scale=scale), reads=["fw"], writes=[dst_r])

                        wrapped_sin(0.25, TWO_PI, cos2[:, :, 0:32], "cos2")
                        P.op("dve", lambda e: e.tensor_copy(out=cos2[:, :, 32:64], in_=cos2[:, :, 0:32]), reads=["cos2"], writes=["cos2"])
                        wrapped_sin(0.0, TWO_PI, ssgn[:, :, 32:64], "ssgn")
                        P.op("dve", lambda e: e.tensor_scalar(out=ssgn[:, :, 0:32], in0=ssgn[:, :, 32:64], scalar1=-1.0, scalar2=None, op0=ALU.mult),
                             reads=["ssgn"], writes=["ssgn"])
                        P.barrier()

                    rsx = {}
                    rskv = {}
                    rsq = {}

                    def S1(j):
                        x_, x_r = xt[j % 3], ("xt", j % 3)
                        P.op("sp", lambda e: e.dma_start(out=x_[:], in_=xin[j * 128:(j + 1) * 128, :]), writes=[x_r], dma=True)
                        rsx[j] = rstd_of(x_[:], [x_r], D, D)

                    def S2(j, part):
                        own = j >= OWN0
                        x_, x_r = xt[j % 3], ("xt", j % 3)
                        h_, h_r = hTt[j % 2], ("hTt", j % 2)
                        pk, pk_r = ps_akv[j % 2], ("ps_akv", j % 2)
                        if part == 0:
                            norm_apply(x_[:], [x_r], rsx[j][0], rsx[j][1], h_[:], [h_r], ps_tr, "ps_tr")
                            return
                        for c in range(8):
                            P.op("pe", lambda e, c=c: e.matmul(pk[:, 0:384], lhsT=h_[:, c, :], rhs=w_a_kv[:, c, :], start=(c == 0), stop=(c == 7)),
                                 reads=[h_r, "w_a_kv"], writes=[pk_r])
                        if own:
                            for c in range(8):
                                P.op("pe", lambda e, c=c: e.matmul(ps_aq[:, 0:384], lhsT=h_[:, c, :], rhs=w_a_q[:, c, :], start=(c == 0), stop=(c == 7)),
                                     reads=[h_r, "w_a_q"], writes=["ps_aq"])
                        rskv[j] = rstd_of(pk[:, 0:256], [pk_r], 256, 256)
                        if own:
                            rsq[j] = rstd_of(ps_aq[:, 0:384], ["ps_aq"], 384, 384)

                    def S3(j, part):
                        own = j >= OWN0
                        pk, pk_r = ps_akv[j % 2], ("ps_akv", j % 2)
                        s_, s_r = akv_s[j % 2], ("akv_s", j % 2)
                        rs, rs_r = rskv[j]
                        t0, t1 = rtmp[0], rtmp[1]
                        if own:
                            t = j - OWN0
                            q_, q_r = aq_s[t % 2], ("aq_s", t % 2)
                        if part == 1:
                            if own:
                                S3b(j, t, t0, t1)
                            return
                        if part == 2:
                            if own:
                                S3c(j, t)
                            return
                        P.op("act", lambda e: e.activation(out=s_[:, 0:256], in_=pk[:, 0:256], func=AF.Copy, scale=rs),
                             reads=[pk_r, rs_r], writes=[s_r])
                        P.op("dve", lambda e: e.tensor_tensor(out=t0[:, 0:64], in0=pk[:, 256:320], in1=cos2[:, j, :], op=ALU.mult),
                             reads=[pk_r, "cos2"], writes=["rtmp0"])
                        P.op("dve", lambda e: e.tensor_tensor(out=t1[:, 0:64], in0=pk[:, 320:384], in1=ssgn[:, j, :], op=ALU.mult),
                             reads=[pk_r, "ssgn"], writes=["rtmp1"])
                        P.op("dve", lambda e: e.tensor_tensor(out=s_[:, 256:320], in0=t0[:, 0:64], in1=t1[:, 0:64], op=ALU.add),
                             reads=["rtmp0", "rtmp1"], writes=[s_r])
                        P.op("dve", lambda e: e.tensor_copy(out=s_[:, 448:512], in_=s_[:, 256:320]), reads=[s_r], writes=[s_r])
                        if own:
                            rq, rq_r = rsq[j]
                            P.op("act", lambda e: e.activation(out=q_[:], in_=ps_aq[:, 0:384], func=AF.Copy, scale=rq),
                                 reads=["ps_aq", rq_r], writes=[q_r])
                        for c in range(4):
                            P.op("pe", lambda e, c=c: e.transpose(out=ps_t2[:, c, :], in_=s_[:, c * 128:(c + 1) * 128], identity=identb[:]),
                                 reads=[s_r, "identb"], writes=["ps_t2"])
                        if own:
                            for c in range(3):
                                P.op("pe", lambda e, c=c: e.transpose(out=ps_t2[:, 4 + c, :], in_=q_[:, c * 128:(c + 1) * 128], identity=identb[:]),
                                     reads=[q_r, "identb"], writes=["ps_t2"])
                        P.op("dve", lambda e: e.tensor_tensor(out=ckvnT[:, :, j * 128:(j + 1) * 128], in0=ps_t2[:, 0:2, :], in1=gBkv[:], op=ALU.mult),
                             reads=["ps_t2", "gBkv"], writes=["ckvnT"])
                        P.op("dve", lambda e: e.tensor_copy(out=krT[:, :, j * 128:(j + 1) * 128], in_=ps_t2[:, 2:4, :]),
                             reads=["ps_t2"], writes=["krT"])
                        if not own:
                            return
                        P.op("dve", lambda e: e.tensor_tensor(out=cqnT[:, :, t * 128:(t + 1) * 128], in0=ps_t2[:, 4:7, :], in1=gBq[:], op=ALU.mult),
                             reads=["ps_t2", "gBq"], writes=[("cqnT", t)])

                    def S3b(j, t, t0, t1):
                        for c in range(3):
                            P.op("pe", lambda e, c=c: e.matmul(ps_qr[:], lhsT=cqnT[:, c, t * 128:(t + 1) * 128], rhs=w_uq_r[:, c, :],
                                                               start=(c == 0), stop=(c == 2)),
                                 reads=[("cqnT", t), "w_uq_r"], writes=["ps_qr"])
                        for c in range(3):
                            P.op("pe", lambda e, c=c: e.matmul(ps_qrs[:], lhsT=cqnT[:, c, t * 128:(t + 1) * 128], rhs=w_uq_rs[:, c, :],
                                                               start=(c == 0), stop=(c == 2)),
                                 reads=[("cqnT", t), "w_uq_rs"], writes=["ps_qrs"])
                        P.op("dve", lambda e: e.tensor_tensor(out=t0[:].rearrange("p (h r) -> p h r", h=8),
                                                              in0=ps_qr[:].rearrange("p (h r) -> p h r", h=8),
                                                              in1=cos2[:, j, :].unsqueeze(1).to_broadcast([128, 8, 64]), op=ALU.mult),
                             reads=["ps_qr", "cos2"], writes=["rtmp0"])
                        P.op("dve", lambda e: e.tensor_tensor(out=t1[:].rearrange("p (h r) -> p h r", h=8),
                                                              in0=ps_qrs[:].rearrange("p (h r) -> p h r", h=8),
                                                              in1=ssgn[:, j, :].unsqueeze(1).to_broadcast([128, 8, 64]), op=ALU.mult),
                             reads=["ps_qrs", "ssgn"], writes=["rtmp1"])
                        P.op("dve", lambda e: e.tensor_tensor(out=qrr[:], in0=t0[:], in1=t1[:], op=ALU.add),
                             reads=["rtmp0", "rtmp1"], writes=["qrr"])

                    def S3c(j, t):
                        for c in range(4):
                            P.op("pe", lambda e, c=c: e.transpose(out=ps_t3[:, c, :], in_=qrr[:, c * 128:(c + 1) * 128], identity=identb[:]),
                                 reads=["qrr", "identb"], writes=["ps_t3"])
                        P.op("act", lambda e: e.activation(out=qrT[:, :, t * 128:(t + 1) * 128], in_=ps_t3[:, 0:4, :], func=AF.Copy),
                             reads=["ps_t3"], writes=["qrT"])

                    for i in range(NCTX + 2):
                        has3 = i >= 2
                        has2 = 1 <= i <= NCTX
                        if has3:
                            S3(i - 2, 0)
                        if has2:
                            S2(i - 1, 0)
                        if has3:
                            S3(i - 2, 1)
                        if has2:
                            S2(i - 1, 1)
                        if has3:
                            S3(i - 2, 2)
                        if i < NCTX:
                            S1(i)
                    P.barrier()
                with ExitStack() as pb:
                    w_ukv_k = sbuf(pb, "w_ukv_k", [128, 2, 1024], BF16)
                    w_ukv_v = sbuf(pb, "w_ukv_v", [128, 2, 1024], BF16)
                    w_uq_n = sbuf(pb, "w_uq_n", [128, 3, 1024], BF16)
                    KT = [sbuf(pb, "KT", [128, NCTX * 128], BF16) for _ in range(2)]
                    Vh = [sbuf(pb, "Vh", [128, NCTX, 128], BF16) for _ in range(2)]
                    qn = [sbuf(pb, "qn", [128, TOWN], BF16) for _ in range(2)]
                    NPT = 4
                    PT = [sbuf(pb, "PT", [128, 512], BF16) for _ in range(NPT)]
                    rsum = [sbuf(pb, "rsum", [128, 512], F32) for _ in range(2)]
                    PS2 = [sbuf(pb, "PS2", [128, 512], BF16) for _ in range(2)]
                    NPS = 3
                    ps_s = [psum(pb, "ps_s", [128, 512]) for _ in range(NPS)]
                    ps_o = [psum(pb, "ps_o", [128, 512]) for _ in range(2)]
                    ps_m = [psum(pb, "ps_m", [128, 512]) for _ in range(2)]
                    ps_gen = psum(pb, "ps_gen", [128, 512])
                    for dst, src, r in ((w_ukv_k, w_ukv_k_d, "w_ukv_k"), (w_ukv_v, w_ukv_v_d, "w_ukv_v"), (w_uq_n, w_uq_n_d, "w_uq_n")):
                        P.op("pool", lambda e, dst=dst, src=src: e.dma_start(out=dst[:], in_=src), writes=[r], dma=True)

                    for l_ in range(2):
                        for f in range(NF):
                            P.op("pool", lambda e, l_=l_, f=f: e.dma_start(out=wgu_bf[l_, f].rearrange("p a c j -> p a (c j)"),
                                                                            in_=wgu_d[l_, f].rearrange("p a c j -> p a (c j)")),
                                 writes=[("wgu_bf", l_, f)], dma=True)
                        for f2 in range(NF // 2):
                            P.op("pool", lambda e, l_=l_, f2=f2: e.dma_start(out=wd_bf[l_, :, 2 * f2:2 * f2 + 2, :],
                                                                              in_=wd_d[l_, :, 2 * f2:2 * f2 + 2, :]),
                                 writes=[("wd_bf", l_, f2)], dma=True)

                    def gen(h):
                        KT_, KT_r = KT[h % 2], ("KT", h % 2)
                        V_, V_r = Vh[h % 2], ("Vh", h % 2)
                        qn_, qn_r = qn[h % 2], ("qn", h % 2)
                        for g in range(NCTX // 4):
                            for c in range(2):
                                P.op("pe", lambda e, g=g, c=c: e.matmul(ps_gen[:], lhsT=w_ukv_k[:, c, h * 128:(h + 1) * 128],
                                                                        rhs=ckvnT[:, c, g * 512:(g + 1) * 512], start=(c == 0), stop=(c == 1)),
                                     reads=["w_ukv_k", "ckvnT"], writes=["ps_gen"])
                            P.op("dve", lambda e, g=g: e.tensor_copy(out=KT_[:, g * 512:(g + 1) * 512], in_=ps_gen[:]),
                                 reads=["ps_gen"], writes=[KT_r])
                            yield
                        for g in range(NCTX // 4):
                            for jj in range(4):
                                j = g * 4 + jj
                                for c in range(2):
                                    P.op("pe", lambda e, j=j, jj=jj, c=c: e.matmul(
                                        ps_gen[:, jj * 128:(jj + 1) * 128], lhsT=ckvnT[:, c, j * 128:(j + 1) * 128],
                                        rhs=w_ukv_v[:, c, h * 128:(h + 1) * 128], start=(c == 0), stop=(c == 1)),
                                        reads=["w_ukv_v", "ckvnT"], writes=["ps_gen"])
                            P.op("dve", lambda e, g=g: e.tensor_copy(out=V_[:, g * 4:(g + 1) * 4, :],
                                                                     in_=ps_gen[:].rearrange("p (a b) -> p a b", a=4)),
                                 reads=["ps_gen"], writes=[V_r])
                            yield
                        for c0 in range(0, TOWN, 512):
                            n = min(512, TOWN - c0)
                            for c in range(3):
                                P.op("pe", lambda e, c=c, c0=c0, n=n: e.matmul(ps_gen[:, 0:n], lhsT=w_uq_n[:, c, h * 128:(h + 1) * 128],
                                                                               rhs=cqnT[:, c, c0:c0 + n], start=(c == 0), stop=(c == 2)),
                                     reads=["w_uq_n"] + [("cqnT", t) for t in range(NOWN)], writes=["ps_gen"])
                            P.op("dve", lambda e, c0=c0, n=n: e.tensor_copy(out=qn_[:, c0:c0 + n], in_=ps_gen[:, 0:n]),
                                 reads=["ps_gen"], writes=[qn_r])
                            yield

                    steps = []
                    first_of_head = {}
                    gi = 0
                    for h in range(8):
                        first_of_head[len(steps)] = h
                        for sb0 in range(0, NOWN, 4):
                            ntile = min(4, NOWN - sb0)
                            nkb = OWN0 + sb0 + ntile
                            for kb in range(nkb):
                                steps.append(dict(h=h, sb0=sb0, W=ntile * 128, nkb=nkb, kb=kb, g=gi))
                            gi += 1

                    def emit_qk(i):
                        s_ = steps[i]
                        h, kb = s_["h"], s_["kb"]
                        idiag = kb - (OWN0 + s_["sb0"])
                        a0 = max(idiag, 0) * 128
                        n = s_["W"] - a0
                        q0 = s_["sb0"] * 128
                        ps_, ps_r = ps_s[i % NPS], ("ps_s", i % NPS)
                        KT_, qn_ = KT[h % 2], qn[h % 2]
                        P.op("pe", lambda e: e.matmul(ps_[:, 0:n], lhsT=KT_[:, kb * 128:(kb + 1) * 128],
                                                      rhs=qn_[:, q0 + a0:q0 + a0 + n], start=True, stop=False),
                             reads=[("KT", h % 2), ("qn", h % 2)], writes=[ps_r])
                        P.op("pe", lambda e: e.matmul(ps_[:, 0:n], lhsT=krT[:, h % 2, kb * 128:(kb + 1) * 128],
                                                      rhs=qrT[:, h // 2, q0 + a0:q0 + a0 + n], start=False, stop=True),
                             reads=["krT", "qrT"], writes=[ps_r])

                    def emit_rest(i):
                        s_ = steps[i]
                        h, kb, nkb, W, g = s_["h"], s_["kb"], s_["nkb"], s_["W"], s_["g"]
                        idiag = kb - (OWN0 + s_["sb0"])
                        a0 = max(idiag, 0) * 128
                        n = W - a0
                        q0 = s_["sb0"] * 128
                        ps_, ps_r = ps_s[i % NPS], ("ps_s", i % NPS)
                        pt_, pt_r = PT[i % NPT], ("PT", i % NPT)
                        po, po_r = ps_o[g % 2], ("ps_o", g % 2)
                        pm, pm_r = ps_m[g % 2], ("ps_m", g % 2)
                        V_ = Vh[h % 2]
                        P.op("act", lambda e: e.activation(out=pt_[:, 0:n], in_=ps_[:, 0:n], func=AF.Exp, scale=MLA_SCALE, bias=mbt[:, kb:kb + 1]),
                             reads=[ps_r, "mbt"], writes=[pt_r])
                        if idiag >= 0:
                            P.op("dve", lambda e: e.memset(pt_[64:128, 0:64], 0.0), writes=[pt_r])
                        P.op("pe", lambda e: e.matmul(po[:, a0:a0 + n], lhsT=V_[:, kb, :], rhs=pt_[:, 0:n], start=(kb == 0), stop=(kb == nkb - 1)),
                             reads=[("Vh", h % 2), pt_r], writes=[po_r])
                        if idiag < 0 and kb % 2 == 0:
                            pass
                        elif idiag < 0:
                            pprev, pprev_r = PT[(i - 1) % NPT], ("PT", (i - 1) % NPT)
                            p2, p2_r = PS2[(i // 2) % 2], ("PS2", (i // 2) % 2)
                            assert steps[i - 1]["g"] == g and steps[i - 1]["kb"] == kb - 1
                            P.op("dve", lambda e: e.tensor_tensor(out=p2[:, 0:W], in0=pprev[:, 0:W], in1=pt_[:, 0:W], op=ALU.add),
                                 reads=[pprev_r, pt_r], writes=[p2_r])
                            P.op("pe", lambda e: e.matmul(pm[:, 0:W], lhsT=onesb[:], rhs=p2[:, 0:W], start=(kb == 1), stop=False),
                                 reads=["onesb", p2_r], writes=[pm_r])
                        else:
                            P.op("pe", lambda e: e.matmul(pm[:, a0:a0 + n], lhsT=onesb[:], rhs=pt_[:, 0:n], start=False, stop=(kb == nkb - 1)),
                                 reads=["onesb", pt_r], writes=[pm_r])
                        if kb == nkb - 1:
                            rs_, rs_r = rsum[g % 2], ("rsum", g % 2)
                            P.op("dve", lambda e: e.reciprocal(out=rs_[:, 0:W], in_=pm[:, 0:W]), reads=[pm_r], writes=[rs_r])
                            P.op("dve", lambda e: e.tensor_tensor(out=oT[:, h, q0:q0 + W], in0=po[:, 0:W], in1=rs_[:, 0:W], op=ALU.mult),
                                 reads=[po_r, rs_r], writes=["oT"])

                    LOOK = 2
                    for _ in gen(0):
                        pass
                    pending_gen = {}
                    for i0, h in first_of_head.items():
                        if h + 1 < 8:
                            pending_gen[i0 + 8] = h + 1
                    cur_gen = None
                    for i in range(len(steps) + LOOK):
                        if i < len(steps):
                            emit_qk(i)
                        if i in pending_gen:
                            cur_gen = gen(pending_gen[i])
                        if cur_gen is not None and i % 4 == 0:
                            if next(cur_gen, "done") == "done":
                                cur_gen = None
                        if i >= LOOK:
                            emit_rest(i - LOOK)
                    assert cur_gen is None
                    P.barrier()
            x1 = sbuf(top, "x1", [128, NOWN, D], F32)
            with ExitStack() as pc:
                w_o = sbuf(pc, "w_o", [128, 8, D], BF16, side="right")
                xr = [sbuf(pc, "xr", [128, D], F32) for _ in range(2)]
                ps_y = [psum(pc, "ps_y", [128, D]) for _ in range(2)]
                P.op("pool", lambda e: e.dma_start(out=w_o[:], in_=w_o_d), writes=["w_o"], dma=True)
                load_gT(0)
                for t in range(NOWN):
                    xr_, xr_r = xr[t % 2], ("xr", t % 2)
                    py, py_r = ps_y[t % 2], ("ps_y", t % 2)
                    P.op("sp", lambda e, xr_=xr_, t=t: e.dma_start(out=xr_[:], in_=xin[(OWN0 + t) * 128:(OWN0 + t + 1) * 128, :]),
                         writes=[xr_r], dma=True)
                    for hf in range(2):
                        for h in range(8):
                            P.op("pe", lambda e, py=py, t=t, hf=hf, h=h: e.matmul(
                                py[:, hf * 512:(hf + 1) * 512], lhsT=oT[:, h, t * 128:(t + 1) * 128],
                                rhs=w_o[:, h, hf * 512:(hf + 1) * 512], start=(h == 0), stop=(h == 7)),
                                reads=["oT", "w_o"], writes=[py_r])
                    post_norm_residual(py[:], py_r, x1[:, t, :], [("x1", t)], xr_[:], [xr_r])
                P.barrier()

        def dump_and_finish():
            for t in range(NOWN):
                P.op("sp", lambda e, t=t: e.dma_start(out=out_d[t * 128:(t + 1) * 128, :], in_=x1[:, t, :]),
                     reads=[("x1", t)], writes=[("out", t)], dma=True)

        ps_tr_top = None
        if stop == "C":
            dump_and_finish()
        else:
            with ExitStack() as rest:
                ps_tr_top = psum(rest, "ps_trt", [128, 8, 128], BF16)
                ffn(rest, x1, 0, C_GPRE + 8, 1, ps_tr_top, final_out=False)
                if stop == "D":
                    dump_and_finish()
                else:
                    with ExitStack() as pf:
                        w_kk = sbuf(pf, "w_kk", [128, 8, 1024], BF16)
                        w_kv_ = sbuf(pf, "w_kvv", [128, 8, 1024], BF16)
                        w_q2 = sbuf(pf, "w_q2", [128, 8, 1024], BF16)
                        w_o2 = sbuf(pf, "w_o2", [128, 8, 1024], BF16)
                        NR = 6
                        K2T = sbuf(pf, "K2T", [128, 8, NR * 128], BF16)
                        V2 = sbuf(pf, "V2", [128, NR, 1024], BF16)
                        hb = sbuf(pf, "hb", [128, 8, 256], BF16)
                        oT2 = sbuf(pf, "oT2", [128, 8, 256], BF16)
                        QP = sbuf(pf, "QP", [128, 8, 2, 2, 128], BF16)
                        NPT = 4
                        PT = [sbuf(pf, "PT2", [128, 512], BF16) for _ in range(NPT)]
                        Bt = [sbuf(pf, "Bt", [128, 5, 2, 128], BF16) for _ in range(2)]
                        rsum = sbuf(pf, "rsum2", [128, 512], F32)
                        rstdF = sbuf(pf, "rstdF", [128, NOWN], F32)
                        ps_gy = psum(pf, "ps_gy", [128, D])
                        ps_s2 = psum(pf, "ps_s2", [128, 512])
                        ps_om = psum(pf, "ps_om", [128, 2048])
                        ps_sr = [ps_gy[:, 0:512], ps_gy[:, 512:1024], ps_s2[:]]
                        GY = [("ps_s2", 0), ("ps_s2", 1)]
                        ps_o = [ps_om[:, 0:512], ps_om[:, 512:1024]]
                        ps_m = [ps_om[:, 1024:1536], ps_om[:, 1536:2048]]
                        ybuf = [(ps_gy[:], GY), (ps_om[:, 0:1024], [("ps_o2", 0), ("ps_o2", 1)])]
                        for dst, src, r in ((w_kk, w_kvs_k_d, "w_kk"), (w_kv_, w_kvs_v_d, "w_kvv"), (w_q2, w_q2_d, "w_q2"), (w_o2, w_o2_d, "w_o2")):
                            P.op("pool", lambda e, dst=dst, src=src: e.dma_start(out=dst[:], in_=src), writes=[r], dma=True)
                        for w_, r, col in ((w_kk, "w_kk", C_GPRE + 16), (w_kv_, "w_kvv", C_GPRE + 16), (w_q2, "w_q2", C_GPRE + 24)):
                            for c in range(8):
                                P.op("dve", lambda e, w_=w_, c=c, col=col: e.tensor_scalar(out=w_[:, c, :], in0=w_[:, c, :],
                                                                                           scalar1=consts[:, col + c:col + c + 1], scalar2=None, op0=ALU.mult),
                                     reads=[r, "consts"], writes=[r])
                        load_gT(2)
                        P.op("dve", lambda e: e.memset(QP[:], 0.0), writes=["QP"])
                        for t in range(NOWN):
                            rs, rs_r = rstd_of(x1[:, t, :], [("x1", t)], D, D)
                            P.op("dve", lambda e, t=t, rs=rs: e.tensor_copy(out=rstdF[:, t:t + 1], in_=rs), reads=[rs_r], writes=[("rstdF", t)])
                        gstep = [0]
                        gpair = [0]
                        yk = [0]

                        def stage_a(sb0):
                            for tt in range(2):
                                t = sb0 + tt
                                xs = xs_ring[xs_i[0] % 2]
                                xs_r = ("xs", xs_i[0] % 2)
                                xs_i[0] += 1
                                P.op("act", lambda e, xs=xs, t=t: e.activation(out=xs[:], in_=x1[:, t, :], func=AF.Copy, scale=rstdF[:, t:t + 1]),
                                     reads=[("x1", t), ("rstdF", t)], writes=[xs_r])
                                for c in range(8):
                                    P.op("pe", lambda e, xs=xs, c=c: e.transpose(out=ps_tr_top[:, c, :], in_=xs[:, c * 128:(c + 1) * 128], identity=identb[:]),
                                         reads=[xs_r, "identb"], writes=["ps_tr"])
                                P.op("dve", lambda e, tt=tt: e.tensor_copy(out=hb[:, :, tt * 128:(tt + 1) * 128], in_=ps_tr_top[:]),
                                     reads=["ps_tr"], writes=["hb"])

                        def stage_b(sb0):
                            slot0 = sb0 % NR
                            for pr in range(8):
                                for c in range(8):
                                    P.op("pe", lambda e, pr=pr, c=c: e.matmul(ps_gy[:, 0:256], lhsT=w_kk[:, c, pr * 128:(pr + 1) * 128],
                                                                              rhs=hb[:, c, :], start=(c == 0), stop=(c == 7)),
                                         reads=["w_kk", "hb"], writes=[GY[0]])
                                P.op("dve", lambda e, pr=pr: e.tensor_copy(out=K2T[:, pr, slot0 * 128:(slot0 + 2) * 128], in_=ps_gy[:, 0:256]),
                                     reads=[GY[0]], writes=["K2T"])
                                for c in range(8):
                                    P.op("pe", lambda e, pr=pr, c=c: e.matmul(ps_gy[:, 512:768], lhsT=w_q2[:, c, pr * 128:(pr + 1) * 128],
                                                                              rhs=hb[:, c, :], start=(c == 0), stop=(c == 7)),
                                         reads=["w_q2", "hb"], writes=[GY[1]])
                                for hh in range(2):
                                    P.op("act", lambda e, pr=pr, hh=hh: e.activation(
                                        out=QP[hh * 64:(hh + 1) * 64, pr, :, hh, :],
                                        in_=ps_gy[hh * 64:(hh + 1) * 64, 512:768].rearrange("p (t q) -> p t q", q=128), func=AF.Copy, scale=0.125),
                                        reads=[GY[1]], writes=["QP"])
                            for tt in range(2):
                                for hf in range(2):
                                    for c in range(8):
                                        P.op("pe", lambda e, tt=tt, hf=hf, c=c: e.matmul(
                                            ps_gy[:, hf * 512:(hf + 1) * 512], lhsT=hb[:, c, tt * 128:(tt + 1) * 128],
                                            rhs=w_kv_[:, c, hf * 512:(hf + 1) * 512], start=(c == 0), stop=(c == 7)),
                                            reads=["w_kvv", "hb"], writes=[GY[hf]])
                                P.op("dve", lambda e, tt=tt: e.tensor_copy(out=V2[:, slot0 + tt, :], in_=ps_gy[:]), reads=GY, writes=["V2"])

                        def stage_c(sb0):
                            kb_lo = max(0, sb0 - 4)
                            kbs = list(range(kb_lo, sb0 + 2))
                            steps = [(pr, ki_) for pr in range(8) for ki_ in range(len(kbs))]

                            def geo(kb):
                                js = [j for j in (sb0, sb0 + 1) if 0 <= j - kb <= 4]
                                return (js[0] - sb0), len(js), js[0] - kb

                            def qk(i, gs):
                                pr, ki_ = steps[i]
                                kb = kbs[ki_]
                                t0_, nt, d0 = geo(kb)
                                slot = kb % NR
                                ps_, ps_r = ps_sr[gs % 3], ("ps_s2", gs % 3)
                                b_, b_r = Bt[(gpair[0] + pr) % 2], ("Bt", (gpair[0] + pr) % 2)
                                if ki_ == 0:
                                    P.op("pool", lambda e: e.dma_start(out=b_[:], in_=btab_d[pr]), writes=[b_r], dma=True)
                                ncol = nt * 256
                                P.op("pe", lambda e: e.matmul(ps_[:, 0:ncol], lhsT=K2T[:, pr, slot * 128:(slot + 1) * 128],
                                                              rhs=QP[:, pr, t0_:t0_ + nt, :, :], start=True, stop=False),
                                     reads=["K2T", "QP"], writes=[ps_r])
                                P.op("pe", lambda e: e.matmul(ps_[:, 0:ncol], lhsT=identb[:], rhs=b_[:, d0:d0 + nt, :, :], start=False, stop=True),
                                     reads=["identb", b_r], writes=[ps_r])

                            def rest(i, gs):
                                pr, ki_ = steps[i]
                                kb = kbs[ki_]
                                t0_, nt, d0 = geo(kb)
                                slot = kb % NR
                                ncol = nt * 256
                                ps_, ps_r = ps_sr[gs % 3], ("ps_s2", gs % 3)
                                pt_, pt_r = PT[gs % NPT], ("PT2", gs % NPT)
                                gp = gpair[0] + pr
                                po, po_r = ps_o[gp % 2], ("ps_o2", gp % 2)
                                pm, pm_r = ps_m[gp % 2], ("ps_m2", gp % 2)
                                first, last = (ki_ == 0), (ki_ == len(kbs) - 1)
                                P.op("act", lambda e: e.activation(out=pt_[:, 0:ncol], in_=ps_[:, 0:ncol], func=AF.Exp), reads=[ps_r], writes=[pt_r])
                                ptv = pt_[:, 0:ncol].rearrange("p (t h q) -> p t h q", h=2, q=128)
                                for hh in range(2):
                                    h = 2 * pr + hh
                                    P.op("pe", lambda e, hh=hh, h=h: e.matmul(
                                        po[hh * 64:(hh + 1) * 64, t0_ * 128:(t0_ + nt) * 128].rearrange("p (t q) -> p t q", q=128),
                                        lhsT=V2[:, slot, h * 64:(h + 1) * 64], rhs=ptv[:, :, hh, :], start=first, stop=last, skip_group_check=True),
                                        reads=["V2", pt_r], writes=[po_r])
                                P.op("pe", lambda e: e.matmul(pm[:, t0_ * 256:t0_ * 256 + ncol], lhsT=onesb[:], rhs=pt_[:, 0:ncol], start=first, stop=last),
                                     reads=["onesb", pt_r], writes=[pm_r])
                                if last:
                                    P.op("dve", lambda e: e.reciprocal(out=rsum[:], in_=pm), reads=[pm_r], writes=["rsum2"])
                                    rsv = rsum[:].rearrange("p (t h q) -> p t h q", h=2, q=128)
                                    for hh in range(2):
                                        P.op("dve", lambda e, hh=hh: e.tensor_tensor(
                                            out=oT2[hh * 64:(hh + 1) * 64, pr, :].rearrange("p (t q) -> p t q", q=128),
                                            in0=po[hh * 64:(hh + 1) * 64, 0:256].rearrange("p (t q) -> p t q", q=128),
                                            in1=rsv[hh * 64:(hh + 1) * 64, :, hh, :], op=ALU.mult),
                                            reads=[po_r, "rsum2"], writes=["oT2"])

                            LOOK = 2
                            for i in range(len(steps) + LOOK):
                                if i < len(steps):
                                    qk(i, gstep[0] + i)
                                if i >= LOOK:
                                    rest(i - LOOK, gstep[0] + i - LOOK)
                            gstep[0] += len(steps)
                            gpair[0] += 8

                        def stage_d(sb0, tt):
                            py, py_r = ybuf[yk[0] % 2]
                            yk[0] += 1
                            for hf in range(2):
                                for pr in range(8):
                                    P.op("pe", lambda e, py=py, tt=tt, hf=hf, pr=pr: e.matmul(
                                        py[:, hf * 512:(hf + 1) * 512], lhsT=oT2[:, pr, tt * 128:(tt + 1) * 128],
                                        rhs=w_o2[:, pr, hf * 512:(hf + 1) * 512], start=(pr == 0), stop=(pr == 7)),
                                        reads=["oT2", "w_o2"], writes=py_r)
                            t = sb0 + tt
                            post_norm_residual(py, py_r, x1[:, t, :], [("x1", t)], x1[:, t, :], [("x1", t)])

                        stage_a(0)
                        stage_b(0)
                        for sb0 in range(0, NOWN, 2):
                            stage_c(sb0)
                            nxt = sb0 + 2 < NOWN
                            if nxt:
                                stage_a(sb0 + 2)
                            stage_d(sb0, 0)
                            stage_d(sb0, 1)
                            if nxt:
                                stage_b(sb0 + 2)
                        P.barrier()
                    if stop == "F":
                        dump_and_finish()
                    else:
                        ffn(rest, x1, 1, C_GPRE + 32, 3, ps_tr_top, final_out=True)
        P.op("sp", lambda e: None, reads=[("out", t) for t in range(NOWN)])
        P.emit(top)
    return nc


def _kc(w):
    k, n = w.shape
    return np.ascontiguousarray(w.reshape(k // 128, 128, n).transpose(1, 0, 2))


def _prep(inputs):
    f = lambda a: np.asarray(a, dtype=np.float32)
    x = f(inputs["x"])
    positions = np.asarray(inputs["positions"]).astype(np.int32)
    w_a = f(inputs["mla_w_a"])[0]
    w_uq = f(inputs["mla_w_uq"])[0]
    w_ukv = f(inputs["mla_w_ukv"])[0]
    sw = (np.arange(64) + 32) % 64
    shared = {}
    shared["w_a_kv"] = _kc(np.concatenate([w_a[:, 384:640], w_a[:, 640:704], w_a[:, 640:704][:, sw]], axis=1))
    shared["w_a_q"] = _kc(w_a[:, 0:384])
    uq = w_uq.reshape(384, 8, 192)
    shared["w_uq_n"] = _kc(np.ascontiguousarray(uq[:, :, 0:128]).reshape(384, 1024))
    shared["w_uq_r"] = _kc(np.ascontiguousarray(uq[:, :, 128:192]).reshape(384, 512))
    shared["w_uq_rs"] = _kc(np.ascontiguousarray(uq[:, :, 128:192][:, :, sw]).reshape(384, 512))
    ukv = w_ukv.reshape(256, 8, 256)
    shared["w_ukv_k"] = _kc(np.ascontiguousarray(ukv[:, :, 0:128]).reshape(256, 1024))
    shared["w_ukv_v"] = _kc(np.ascontiguousarray(ukv[:, :, 128:256]).reshape(256, 1024))
    shared["w_o"] = _kc(f(inputs["mla_w_o"])[0])
    wg = f(inputs["ffn_w_gate"]).reshape(2, 8, 128, NF, 128)
    wu = f(inputs["ffn_w_up"]).reshape(2, 8, 128, NF, 128)
    wgu = np.stack([wg, wu], axis=1)
    shared["wgu"] = np.ascontiguousarray(wgu.transpose(0, 4, 3, 1, 2, 5))
    shared["wd"] = np.ascontiguousarray(f(inputs["ffn_w_down"]).reshape(2, NF, 128, D).transpose(0, 2, 1, 3))
    wkv = f(inputs["w_kv_shared"])
    shared["w_kvs_k"] = _kc(wkv[:, 0:1024])
    shared["w_kvs_v"] = _kc(wkv[:, 1024:2048])
    shared["w_q2"] = _kc(f(inputs["b_w_q"])[0])
    shared["w_o2"] = _kc(f(inputs["b_w_o"])[0])
    rel = f(inputs["b_rel_table"])[0]
    kl = np.arange(128)[:, None]
    cc = np.arange(640)[None, :]
    idx = np.clip(cc - kl, -63, 256) + 63
    bt = rel[:, idx]
    dl = cc // 128
    qh = (cc % 128) >= 64
    kh = kl >= 64
    invalid = ((dl == 0) & (~qh) & kh) | ((dl == 4) & qh & (~kh))
    bt = np.where(invalid[None], np.float32(NEG), bt).astype(np.float32)
    shared["btab"] = np.ascontiguousarray(bt.reshape(8, 2, 128, 5, 128).transpose(0, 2, 3, 1, 4))
    consts = np.zeros((128, NCONST), np.float32)
    pre = [f(inputs["attn_pre_g"])[0], f(inputs["ffn_pre_g"])[0], f(inputs["kv_src_g"]),
           f(inputs["attn_pre_g"])[1], f(inputs["ffn_pre_g"])[1]]
    for i, g in enumerate(pre):
        consts[:, C_GPRE + 8 * i:C_GPRE + 8 * (i + 1)] = g.reshape(8, 128).T
    consts[:, C_GQ:C_GQ + 3] = f(inputs["mla_g_q"])[0].reshape(3, 128).T
    consts[:, C_GKV:C_GKV + 2] = f(inputs["mla_g_kv"])[0].reshape(2, 128).T
    inv = (1.0 / (np.float32(10000.0) ** (np.arange(0, 64, 2, dtype=np.float32) / np.float32(64)))).astype(np.float32)
    consts[:, C_INV:C_INV + 32] = (inv.astype(np.float64) / (2 * np.pi)).astype(np.float32)[None, :]
    consts[:, C_IDENT:C_IDENT + 128] = np.eye(128, dtype=np.float32)
    shared["consts"] = consts
    shared["gpost"] = np.stack([f(inputs["attn_post_g"])[0], f(inputs["ffn_post_g"])[0],
                                f(inputs["attn_post_g"])[1], f(inputs["ffn_post_g"])[1]]).astype(np.float32)
    in_maps = []
    for core in range(8):
        b, half = core // 2, core % 2
        m = dict(shared)
        if half == 0:
            xin = np.concatenate([np.zeros((OWN0 * 128, D), np.float32), x[b, 0:TOWN]], axis=0)
            pos = np.concatenate([np.zeros(OWN0 * 128, np.int32), positions[b, 0:TOWN]])
            mb = np.zeros((128, NCTX), np.float32)
            mb[:, 0:OWN0] = NEG
        else:
            xin = x[b]
            pos = positions[b]
            mb = np.zeros((128, NCTX), np.float32)
        m["xin"] = np.ascontiguousarray(xin)
        m["posT"] = np.ascontiguousarray(pos.reshape(NCTX, 128).T.astype(np.int32))
        m["mb"] = mb
        in_maps.append(m)
    return in_maps


_NC_CACHE = {}


def _run(inputs, stop="G"):
    if stop not in _NC_CACHE:
        _NC_CACHE[stop] = build(stop)
    nc = _NC_CACHE[stop]
    in_maps = _prep(inputs)
    res = run_bass_kernel_spmd(nc, in_maps, core_ids=list(range(8)))
    out = np.zeros((4, 4096, D), np.float32)
    for core in range(8):
        b, half = core // 2, core % 2
        o = np.asarray(res.results[core]["out"], dtype=np.float32)
        if half == 0:
            out[b, 0:TOWN] = o
        else:
            out[b, TOWN:4096] = o[TOWN - (4096 - TOWN):]
    return out


def kernel(**inputs):
    return _run(inputs, "G")
```

```python
import numpy as np
from contextlib import ExitStack
import concourse.bass as bass
import concourse.mybir as mybir
from concourse.bass_utils import run_bass_kernel_spmd

F32 = mybir.dt.float32
BF16 = mybir.dt.bfloat16
I32 = mybir.dt.int32
AF = mybir.ActivationFunctionType
ALU = mybir.AluOpType

D = 1024
NCTX = 32
NOWN = 18
OWN0 = NCTX - NOWN
TOWN = NOWN * 128
DFF = 2816
NF = DFF // 128
EPS = 1e-6
NEG = -30000.0
TWO_PI = float(2 * np.pi * (1 - 1e-6))
MLA_SCALE = float(192 ** -0.5)
ENGS = ("pe", "act", "dve", "pool", "sp")


class Op:
    __slots__ = ("eng", "fn", "deps", "dma", "sig", "sem", "semval", "idx", "prewait")

    def __init__(self, eng, fn, dma):
        self.eng = eng
        self.fn = fn
        self.dma = dma
        self.deps = set()
        self.sig = False
        self.sem = None
        self.semval = 0
        self.prewait = None


class Prog:
    def __init__(self, nc, n_dma_sems=(24, 12)):
        self.nc = nc
        self.ops = []
        self.last_w = {}
        self.readers = {}
        self.n_dma_sems = {"sp": n_dma_sems[0], "pool": n_dma_sems[1]}
        self.dma_since_bar = []
        self.nbar = 0

    def op(self, eng, fn, reads=(), writes=(), dma=False):
        o = Op(eng, fn, dma)
        o.idx = len(self.ops)
        key = ("dma", o.idx) if dma else eng
        for r in reads:
            o.deps.update(self.last_w.get(r, {}).values())
            if self._is_psum(r):
                o.deps.update(i for k, i in self.readers.get(r, {}).items() if k != key)
        for w_ in writes:
            o.deps.update(self.last_w.get(w_, {}).values())
            o.deps.update(self.readers.get(w_, {}).values())
        for r in reads:
            self._put(self.readers.setdefault(r, {}), key, o.idx)
        for w_ in writes:
            self._put(self.last_w.setdefault(w_, {}), key, o.idx)
        o.deps.discard(o.idx)
        self.ops.append(o)
        if dma:
            self.dma_since_bar.append(o.idx)
        return o

    @staticmethod
    def _is_psum(r):
        n = r[0] if isinstance(r, tuple) else r
        return isinstance(n, str) and n.startswith("ps")

    @staticmethod
    def _put(d, key, idx):
        d[key] = idx
        if len(d) > 24:
            for k in sorted((k for k in d if isinstance(k, tuple)), key=lambda k: k[1])[:8]:
                del d[k]

    def barrier(self):
        n = self.nbar
        self.nbar += 1
        sig = []
        for e in ("pe", "act", "dve", "pool"):
            sig.append(self.op(e, lambda eng: eng.drain(), writes=[("bar", n, e)]).idx)
        extra = set(sig) | set(self.dma_since_bar)
        self.dma_since_bar = []
        for e in ENGS:
            o = self.op(e, lambda eng: None)
            o.deps |= extra

    def emit(self, stack):
        nc = self.nc
        ops = self.ops
        for o in ops:
            if o.eng == "pe" and not o.dma:
                o.deps = {d for d in o.deps if not (ops[d].eng == "pe" and not ops[d].dma)}
        for o in ops:
            for d in o.deps:
                ops[d].sig = True
        sems = {e: stack.enter_context(nc.semaphore("c_" + e)) for e in ("pe", "act", "dve", "pool")}
        dsems = {q: [stack.enter_context(nc.semaphore("d_%s%d" % (q, i))) for i in range(n)]
                 for q, n in self.n_dma_sems.items()}
        cnt = {e: 0 for e in ENGS}
        dcnt = {q: 0 for q in dsems}
        duse = {q: [0] * len(dsems[q]) for q in dsems}
        for o in ops:
            if o.dma:
                q = o.eng
                i = dcnt[q] % len(dsems[q])
                dcnt[q] += 1
                o.sem = dsems[q][i]
                if duse[q][i] > 0:
                    o.prewait = (o.sem, 16 * duse[q][i])
                duse[q][i] += 1
                o.semval = 16 * duse[q][i]
            elif o.sig:
                cnt[o.eng] += 1
                o.sem = sems[o.eng]
                o.semval = cnt[o.eng]
        per = {e: [] for e in ENGS}
        for o in ops:
            per[o.eng].append(o)

        def run(ename, eng):
            waited = {}
            for o in per[ename]:
                need = {}
                if o.prewait is not None:
                    need[id(o.prewait[0])] = o.prewait
                for d in o.deps:
                    p = ops[d]
                    k = id(p.sem)
                    if k not in need or need[k][1] < p.semval:
                        need[k] = (p.sem, p.semval)
                for k, (s, v) in need.items():
                    if waited.get(k, 0) >= v:
                        continue
                    eng.wait_ge(s, v)
                    waited[k] = v
                ins = o.fn(eng)
                if ins is None:
                    continue
                if o.dma:
                    ins.then_inc(o.sem, 16)
                elif o.sig:
                    ins.then_inc(o.sem, 1)

        block = stack.enter_context(nc.Block())

        @block.tensor
        def _(e):
            run("pe", e)

        @block.scalar
        def _(e):
            run("act", e)

        @block.vector
        def _(e):
            run("dve", e)

        @block.gpsimd
        def _(e):
            run("pool", e)

        @block.sync
        def _(e):
            run("sp", e)


C_GPRE = 0
C_GQ = 40
C_GKV = 43
C_INV = 45
C_IDENT = 77
NCONST = C_IDENT + 128

STOP_PHASES = ("C", "D", "F", "G")


def build(stop="G"):
    nc = bass.Bass("TRN2", target_bir_lowering=False)

    def din(name, shape, dt=F32):
        return nc.dram_tensor(name, list(shape), dt, kind="ExternalInput").ap()

    xin = din("xin", [NCTX * 128, D])
    posT = din("posT", [128, NCTX], I32)
    mb_d = din("mb", [128, NCTX])
    consts_d = din("consts", [128, NCONST])
    gpost_d = din("gpost", [4, D])
    w_a_kv_d = din("w_a_kv", [128, 8, 384])
    w_a_q_d = din("w_a_q", [128, 8, 384])
    w_uq_n_d = din("w_uq_n", [128, 3, 1024])
    w_uq_r_d = din("w_uq_r", [128, 3, 512])
    w_uq_rs_d = din("w_uq_rs", [128, 3, 512])
    w_ukv_k_d = din("w_ukv_k", [128, 2, 1024])
    w_ukv_v_d = din("w_ukv_v", [128, 2, 1024])
    w_o_d = din("w_o", [128, 8, 1024])
    wgu_d = din("wgu", [2, NF, 128, 2, 8, 128])
    wd_d = din("wd", [2, 128, NF, 1024])
    w_kvs_k_d = din("w_kvs_k", [128, 8, 1024])
    w_kvs_v_d = din("w_kvs_v", [128, 8, 1024])
    w_q2_d = din("w_q2", [128, 8, 1024])
    w_o2_d = din("w_o2", [128, 8, 1024])
    btab_d = din("btab", [8, 128, 5, 2, 128])
    out_d = nc.dram_tensor("out", [TOWN, D], F32, kind="ExternalOutput").ap()
    wgu_bf = nc.dram_tensor("wgu_bf", [2, NF, 128, 2, 8, 128], BF16).ap()
    wd_bf = nc.dram_tensor("wd_bf", [2, 128, NF, 1024], BF16).ap()

    P = Prog(nc)
    uid = [0]

    def nm(s):
        uid[0] += 1
        return "%s_%d" % (s, uid[0])

    with ExitStack() as top:
        def sbuf(st, name, shape, dt, side=None):
            return st.enter_context(nc.sbuf_tensor(nm(name), list(shape), dt, side=side))

        def psum(st, name, shape, dt=F32):
            return st.enter_context(nc.psum_tensor(nm(name), list(shape), dt))

        consts = sbuf(top, "consts", [128, NCONST], F32)
        identb = sbuf(top, "identb", [128, 128], BF16)
        onesb = sbuf(top, "onesb", [128, 128], BF16)
        mbt = sbuf(top, "mbt", [128, NCTX], F32)
        mhalf = sbuf(top, "mhalf", [128, 1], F32)
        stat = sbuf(top, "stat", [128, 64], F32)
        gT = sbuf(top, "gT", [128, D], F32)
        tmpA = sbuf(top, "tmpA", [128, D], F32)
        junk = sbuf(top, "junk", [128, D], BF16)
        xs_ring = [sbuf(top, "xs", [128, D], BF16) for _ in range(2)]
        gB = sbuf(top, "gB", [128, 8, 128], BF16)

        P.op("sp", lambda e: e.dma_start(out=consts[:], in_=consts_d), writes=["consts"], dma=True)
        P.op("sp", lambda e: e.dma_start(out=mbt[:], in_=mb_d), writes=["mbt"], dma=True)
        P.op("dve", lambda e: e.tensor_copy(out=identb[:], in_=consts[:, C_IDENT:C_IDENT + 128]),
             reads=["consts"], writes=["identb"])
        P.op("dve", lambda e: e.memset(onesb[:], 1.0), writes=["onesb"])
        P.op("dve", lambda e: e.memset(mhalf[:], -0.5), writes=["mhalf"])

        stat_i = [0]

        def stat_col():
            i = stat_i[0] % 64
            stat_i[0] += 1
            return stat[:, i:i + 1], ("stat", i)

        def load_gB(col0, nch=8):
            for c in range(nch):
                P.op("dve", lambda e, c=c: e.tensor_scalar(out=gB[:, c, :], in0=onesb[:], scalar1=consts[:, col0 + c:col0 + c + 1],
                                                           scalar2=None, op0=ALU.mult),
                     reads=["onesb", "consts"], writes=["gB"])

        def load_gT(row):
            P.op("sp", lambda e: e.dma_start(out=gT[:], in_=gpost_d[row].partition_broadcast(128)), writes=["gT"], dma=True)

        def rstd_of(src_ap, src_res, width, dim):
            ss, ss_r = stat_col()
            rs, rs_r = stat_col()
            P.op("act", lambda e: e.activation(out=junk[:, 0:width], in_=src_ap, func=AF.Square, accum_out=ss),
                 reads=list(src_res), writes=["junk", ss_r])
            P.op("dve", lambda e: e.tensor_scalar(out=ss, in0=ss, scalar1=1.0 / dim, scalar2=EPS, op0=ALU.mult, op1=ALU.add),
                 reads=[ss_r], writes=[ss_r])
            P.op("pool", lambda e: e.tensor_tensor(out=rs, in0=ss, in1=mhalf[:], op=ALU.pow), reads=[ss_r, "mhalf"], writes=[rs_r])
            return rs, rs_r

        xs_i = [0]

        def norm_apply(src_ap, src_res, rs, rs_r, dst_ap, dst_res, ps_tr, ps_res, nch=8):
            xs = xs_ring[xs_i[0] % 2]
            xs_r = ("xs", xs_i[0] % 2)
            xs_i[0] += 1
            P.op("act", lambda e: e.activation(out=xs[:, 0:nch * 128], in_=src_ap, func=AF.Copy, scale=rs),
                 reads=list(src_res) + [rs_r], writes=[xs_r])
            for c in range(nch):
                P.op("pe", lambda e, c=c: e.transpose(out=ps_tr[:, c, :], in_=xs[:, c * 128:(c + 1) * 128], identity=identb[:]),
                     reads=[xs_r, "identb"], writes=[ps_res])
            P.op("dve", lambda e: e.tensor_tensor(out=dst_ap, in0=ps_tr[:, 0:nch, :], in1=gB[:, 0:nch, :], op=ALU.mult),
                 reads=[ps_res, "gB"], writes=list(dst_res))

        def norm_T(src_ap, src_res, dst_ap, dst_res, ps_tr, ps_res, nch=8):
            rs, rs_r = rstd_of(src_ap, src_res, nch * 128, nch * 128)
            norm_apply(src_ap, src_res, rs, rs_r, dst_ap, dst_res, ps_tr, ps_res, nch)

        def post_norm_residual(ps_y, ps_res, x_dst, x_dst_res, x_src, x_src_res):
            ps_rl = list(ps_res) if isinstance(ps_res, list) else [ps_res]
            rs, rs_r = rstd_of(ps_y, ps_rl, D, D)
            P.op("dve", lambda e: e.scalar_tensor_tensor(out=tmpA[:], in0=ps_y, scalar=rs, in1=gT[:], op0=ALU.mult, op1=ALU.mult),
                 reads=ps_rl + [rs_r, "gT"], writes=["tmpA"])
            P.op("dve", lambda e: e.tensor_tensor(out=x_dst, in0=tmpA[:], in1=x_src, op=ALU.add),
                 reads=["tmpA"] + list(x_src_res), writes=list(x_dst_res))

        def ffn(st_outer, x1, layer, gpre_col, gpost_row, ps_tr, final_out):
            GT = 6
            G = GT * 128
            NG = NOWN // GT
            with ExitStack() as st:
                wd = sbuf(st, "wd", [128, NF, D], BF16)
                actT = sbuf(st, "actT", [128, NF, G], BF16)
                hT = sbuf(st, "hT", [128, 8, G], BF16)
                wgu = [sbuf(st, "wgu", [128, 2, 8, 128], BF16) for _ in range(4)]
                sg = [sbuf(st, "sg", [128, 512], BF16) for _ in range(2)]
                pA = psum(st, "pA", [128, D])
                pB = psum(st, "pB", [128, D])
                pC = psum(st, "pC", [128, D])
                ps_g = [pA[:, 0:512], pB[:, 0:512]]
                ps_u = [pA[:, 512:1024], pB[:, 512:1024]]
                ps_y = [pC, pA]
                ps_y_r = [[("ps_y", 0)], [("ps_g", 0), ("ps_u", 0)]]
                for half in range(2):
                    P.op("sp", lambda e, half=half: e.dma_start(out=wd[:, half * 11:(half + 1) * 11, :],
                                                                 in_=wd_bf[layer, :, half * 11:(half + 1) * 11, :]),
                         reads=[("wd_bf", layer, f2) for f2 in range(NF // 2)], writes=[("wd", half)], dma=True)
                load_gB(gpre_col)
                load_gT(gpost_row)
                kk = [0]
                yk = [0]

                def d1_stats(g0):
                    return [rstd_of(x1[:, g0 + t, :], [("x1", g0 + t)], D, D) for t in range(GT)]

                def d1_apply(g0, t, rr):
                    norm_apply(x1[:, g0 + t, :], [("x1", g0 + t)], rr[0], rr[1], hT[:, :, t * 128:(t + 1) * 128], ["hT"], ps_tr, "ps_tr")

                def d2(g0):
                    for f in range(NF):
                        k = kk[0]
                        w = wgu[k % 4]
                        w_r = ("wgu", k % 4)
                        P.op("sp", lambda e, w=w, f=f: e.dma_start(out=w[:], in_=wgu_bf[layer, f]),
                             reads=[("wgu_bf", layer, f)], writes=[w_r], dma=True)
                        for pi, (c0, c1) in enumerate(((0, 512), (512, G))):
                            j = 2 * k + pi
                            pg, pu = ps_g[j % 2], ps_u[j % 2]
                            pg_r, pu_r = ("ps_g", j % 2), ("ps_u", j % 2)
                            s_, s_r = sg[j % 2], ("sg", j % 2)
                            n = c1 - c0
                            for c in range(8):
                                P.op("pe", lambda e, pg=pg, w=w, c=c, c0=c0, c1=c1, n=n: e.matmul(
                                    pg[:, 0:n], lhsT=w[:, 0, c, :], rhs=hT[:, c, c0:c1], start=(c == 0), stop=(c == 7)),
                                    reads=[w_r, "hT"], writes=[pg_r])
                            for c in range(8):
                                P.op("pe", lambda e, pu=pu, w=w, c=c, c0=c0, c1=c1, n=n: e.matmul(
                                    pu[:, 0:n], lhsT=w[:, 1, c, :], rhs=hT[:, c, c0:c1], start=(c == 0), stop=(c == 7)),
                                    reads=[w_r, "hT"], writes=[pu_r])
                            P.op("act", lambda e, pg=pg, s_=s_, n=n: e.activation(out=s_[:, 0:n], in_=pg[:, 0:n], func=AF.Silu),
                                 reads=[pg_r], writes=[s_r])
                            P.op("dve", lambda e, pu=pu, s_=s_, f=f, c0=c0, c1=c1, n=n: e.tensor_tensor(
                                out=actT[:, f, c0:c1], in0=pu[:, 0:n], in1=s_[:, 0:n], op=ALU.mult),
                                reads=[pu_r, s_r], writes=["actT"])
                        kk[0] += 1

                def d3_tile(g0, t):
                    py = ps_y[yk[0] % 2]
                    py_r = ps_y_r[yk[0] % 2]
                    yk[0] += 1
                    for hf in range(2):
                        for f in range(NF):
                            P.op("pe", lambda e, py=py, t=t, hf=hf, f=f: e.matmul(
                                py[:, hf * 512:(hf + 1) * 512], lhsT=actT[:, f, t * 128:(t + 1) * 128],
                                rhs=wd[:, f, hf * 512:(hf + 1) * 512], start=(f == 0), stop=(f == NF - 1)),
                                reads=["actT", ("wd", f // 11)], writes=py_r)
                    tt = g0 + t
                    post_norm_residual(py[:], py_r, x1[:, tt, :], [("x1", tt)], x1[:, tt, :], [("x1", tt)])
                    if final_out:
                        P.op("sp", lambda e, tt=tt: e.dma_start(out=out_d[tt * 128:(tt + 1) * 128, :], in_=x1[:, tt, :]),
                             reads=[("x1", tt)], writes=[("out", tt)], dma=True)

                rr = d1_stats(0)
                for t in range(GT):
                    d1_apply(0, t, rr[t])
                for gi_ in range(NG):
                    g0 = gi_ * GT
                    d2(g0)
                    if gi_ + 1 < NG:
                        rr = d1_stats(g0 + GT)
                    for t in range(GT):
                        d3_tile(g0, t)
                        if gi_ + 1 < NG:
                            d1_apply(g0 + GT, t, rr[t])
            P.barrier()

        with ExitStack() as mla:
            oT = sbuf(mla, "oT", [128, 8, TOWN], BF16, side="right")
            with ExitStack() as mla_ab:
                ckvnT = sbuf(mla_ab, "ckvnT", [128, 2, NCTX * 128], BF16, side="right")
                krT = sbuf(mla_ab, "krT", [128, 2, NCTX * 128], BF16, side="right")
                cqnT = sbuf(mla_ab, "cqnT", [128, 3, TOWN], BF16, side="right")
                qrT = sbuf(mla_ab, "qrT", [128, 4, TOWN], BF16, side="right")
                with ExitStack() as pa:
                    cos2 = sbuf(pa, "cos2", [128, NCTX, 64], F32)
                    ssgn = sbuf(pa, "ssgn", [128, NCTX, 64], F32)
                    w_a_kv = sbuf(pa, "w_a_kv", [128, 8, 384], BF16)
                    w_a_q = sbuf(pa, "w_a_q", [128, 8, 384], BF16)
                    w_uq_r = sbuf(pa, "w_uq_r", [128, 3, 512], BF16)
                    w_uq_rs = sbuf(pa, "w_uq_rs", [128, 3, 512], BF16)
                    xt = [sbuf(pa, "xt", [128, D], F32) for _ in range(3)]
                    hTt = [sbuf(pa, "hTt", [128, 8, 128], BF16) for _ in range(2)]
                    akv_s = [sbuf(pa, "akv_s", [128, 512], BF16) for _ in range(2)]
                    aq_s = [sbuf(pa, "aq_s", [128, 384], BF16) for _ in range(2)]
                    gBkv = sbuf(pa, "gBkv", [128, 2, 128], BF16)
                    gBq = sbuf(pa, "gBq", [128, 3, 128], BF16)
                    rtmp = [sbuf(pa, "rtmp", [128, 512], F32) for _ in range(2)]
                    qrr = sbuf(pa, "qrr", [128, 512], BF16)
                    ps_tr = psum(pa, "ps_tr", [128, 8, 128], BF16)
                    ps_akv = [psum(pa, "ps_akv", [128, 512]) for _ in range(2)]
                    ps_aq = psum(pa, "ps_aq", [128, 512])
                    ps_t2 = psum(pa, "ps_t2", [128, 8, 128], BF16)
                    ps_qr = psum(pa, "ps_qr", [128, 512])
                    ps_qrs = psum(pa, "ps_qrs", [128, 512])
                    ps_t3 = psum(pa, "ps_t3", [128, 8, 128], BF16)

                    for dst, src, r in ((w_a_kv, w_a_kv_d, "w_a_kv"), (w_a_q, w_a_q_d, "w_a_q"),
                                        (w_uq_r, w_uq_r_d, "w_uq_r"), (w_uq_rs, w_uq_rs_d, "w_uq_rs")):
                        P.op("pool", lambda e, dst=dst, src=src: e.dma_start(out=dst[:], in_=src), writes=[r], dma=True)
                    load_gB(C_GPRE + 0)
                    for i_ in range(2):
                        P.op("dve", lambda e, i_=i_: e.memset(akv_s[i_][:, 320:448], 0.0), writes=[("akv_s", i_)])
                    for c in range(2):
                        P.op("dve", lambda e, c=c: e.tensor_scalar(out=gBkv[:, c, :], in0=onesb[:], scalar1=consts[:, C_GKV + c:C_GKV + c + 1],
                                                                   scalar2=None, op0=ALU.mult), reads=["onesb", "consts"], writes=["gBkv"])
                    for c in range(3):
                        P.op("dve", lambda e, c=c: e.tensor_scalar(out=gBq[:, c, :], in0=onesb[:], scalar1=consts[:, C_GQ + c:C_GQ + c + 1],
                                                                   scalar2=None, op0=ALU.mult), reads=["onesb", "consts"], writes=["gBq"])
                    with ExitStack() as rp:
                        posi = sbuf(rp, "posi", [128, NCTX], I32)
                        posf = sbuf(rp, "posf", [128, NCTX], F32)
                        u = sbuf(rp, "u", [128, NCTX, 32], F32)
                        ki = sbuf(rp, "ki", [128, NCTX, 32], I32)
                        kf = sbuf(rp, "kf", [128, NCTX, 32], F32)
                        fw = sbuf(rp, "fw", [128, NCTX, 32], F32)
                        P.op("sp", lambda e: e.dma_start(out=posi[:], in_=posT), writes=["posi"], dma=True)
                        P.op("dve", lambda e: e.tensor_copy(out=posf[:], in_=posi[:]), reads=["posi"], writes=["posf"])
                        P.op("dve", lambda e: e.tensor_tensor(out=u[:], in0=posf[:].unsqueeze(2).to_broadcast([128, NCTX, 32]),
                                                              in1=consts[:, C_INV:C_INV + 32].unsqueeze(1).to_broadcast([128, NCTX, 32]),
                                                              op=ALU.mult), reads=["posf", "consts"], writes=["u"])
                        P.op("dve", lambda e: e.tensor_copy(out=ki[:], in_=u[:]), reads=["u"], writes=["ki"])
                        P.op("dve", lambda e: e.tensor_copy(out=kf[:], in_=ki[:]), reads=["ki"], writes=["kf"])
                        P.op("dve", lambda e: e.tensor_tensor(out=u[:], in0=u[:], in1=kf[:], op=ALU.subtract), reads=["u", "kf"], writes=["u"])

                        def wrapped_sin(shift, scale, dst, dst_r):
                            P.op("dve", lambda e: e.tensor_scalar(out=fw[:], in0=u[:], scalar1=shift, scalar2=None, op0=ALU.add), reads=["u"], writes=["fw"])
                            P.op("dve", lambda e: e.tensor_scalar(out=kf[:], in0=fw[:], scalar1=0.5, scalar2=None, op0=ALU.is_gt), reads=["fw"], writes=["kf"])
                            P.op("dve", lambda e: e.tensor_tensor(out=fw[:], in0=fw[:], in1=kf[:], op=ALU.subtract), reads=["fw", "kf"], writes=["fw"])
                            P.op("dve", lambda e: e.tensor_scalar(out=kf[:], in0=fw[:], scalar1=-0.5, scalar2=None, op0=ALU.is_lt), reads=["fw"], writes=["kf"])
                            P.op("dve", lambda e: e.tensor_tensor(out=fw[:], in0=fw[:], in1=kf[:], op=ALU.add), reads=["fw", "kf"], writes=["fw"])
                            P.op("act", lambda e: e.activation(out=dst, in_=fw[:], func=AF.Sin, scale=scale), reads=["fw"], writes=[dst_r])

                        wrapped_sin(0.25, TWO_PI, cos2[:, :, 0:32], "cos2")
                        P.op("dve", lambda e: e.tensor_copy(out=cos2[:, :, 32:64], in_=cos2[:, :, 0:32]), reads=["cos2"], writes=["cos2"])
                        wrapped_sin(0.0, TWO_PI, ssgn[:, :, 32:64], "ssgn")
                        P.op("dve", lambda e: e.tensor_scalar(out=ssgn[:, :, 0:32], in0=ssgn[:, :, 32:64], scalar1=-1.0, scalar2=None, op0=ALU.mult),
                             reads=["ssgn"], writes=["ssgn"])
                        P.barrier()

                    rsx = {}
                    rskv = {}
                    rsq = {}

                    def S1(j):
                        x_, x_r = xt[j % 3], ("xt", j % 3)
                        P.op("sp", lambda e: e.dma_start(out=x_[:], in_=xin[j * 128:(j + 1) * 128, :]), writes=[x_r], dma=True)
                        rsx[j] = rstd_of(x_[:], [x_r], D, D)

                    def S2(j, part):
                        own = j >= OWN0
                        x_, x_r = xt[j % 3], ("xt", j % 3)
                        h_, h_r = hTt[j % 2], ("hTt", j % 2)
                        pk, pk_r = ps_akv[j % 2], ("ps_akv", j % 2)
                        if part == 0:
                            norm_apply(x_[:], [x_r], rsx[j][0], rsx[j][1], h_[:], [h_r], ps_tr, "ps_tr")
                            return
                        for c in range(8):
                            P.op("pe", lambda e, c=c: e.matmul(pk[:, 0:384], lhsT=h_[:, c, :], rhs=w_a_kv[:, c, :], start=(c == 0), stop=(c == 7)),
                                 reads=[h_r, "w_a_kv"], writes=[pk_r])
                        if own:
                            for c in range(8):
                                P.op("pe", lambda e, c=c: e.matmul(ps_aq[:, 0:384], lhsT=h_[:, c, :], rhs=w_a_q[:, c, :], start=(c == 0), stop=(c == 7)),
                                     reads=[h_r, "w_a_q"], writes=["ps_aq"])
                        rskv[j] = rstd_of(pk[:, 0:256], [pk_r], 256, 256)
                        if own:
                            rsq[j] = rstd_of(ps_aq[:, 0:384], ["ps_aq"], 384, 384)

                    def S3(j, part):
                        own = j >= OWN0
                        pk, pk_r = ps_akv[j % 2], ("ps_akv", j % 2)
                        s_, s_r = akv_s[j % 2], ("akv_s", j % 2)
                        rs, rs_r = rskv[j]
                        t0, t1 = rtmp[0], rtmp[1]
                        if own:
                            t = j - OWN0
                            q_, q_r = aq_s[t % 2], ("aq_s", t % 2)
                        if part == 1:
                            if own:
                                S3b(j, t, t0, t1)
                            return
                        if part == 2:
                            if own:
                                S3c(j, t)
                            return
                        P.op("act", lambda e: e.activation(out=s_[:, 0:256], in_=pk[:, 0:256], func=AF.Copy, scale=rs),
                             reads=[pk_r, rs_r], writes=[s_r])
                        P.op("dve", lambda e: e.tensor_tensor(out=t0[:, 0:64], in0=pk[:, 256:320], in1=cos2[:, j, :], op=ALU.mult),
                             reads=[pk_r, "cos2"], writes=["rtmp0"])
                        P.op("dve", lambda e: e.tensor_tensor(out=t1[:, 0:64], in0=pk[:, 320:384], in1=ssgn[:, j, :], op=ALU.mult),
                             reads=[pk_r, "ssgn"], writes=["rtmp1"])
                        P.op("dve", lambda e: e.tensor_tensor(out=s_[:, 256:320], in0=t0[:, 0:64], in1=t1[:, 0:64], op=ALU.add),
                             reads=["rtmp0", "rtmp1"], writes=[s_r])
                        P.op("dve", lambda e: e.tensor_copy(out=s_[:, 448:512], in_=s_[:, 256:320]), reads=[s_r], writes=[s_r])
                        if own:
                            rq, rq_r = rsq[j]
                            P.op("act", lambda e: e.activation(out=q_[:], in_=ps_aq[:, 0:384], func=AF.Copy, scale=rq),
                                 reads=["ps_aq", rq_r], writes=[q_r])
                        for c in range(4):
                            P.op("pe", lambda e, c=c: e.transpose(out=ps_t2[:, c, :], in_=s_[:, c * 128:(c + 1) * 128], identity=identb[:]),
                                 reads=[s_r, "identb"], writes=["ps_t2"])
                        if own:
                            for c in range(3):
                                P.op("pe", lambda e, c=c: e.transpose(out=ps_t2[:, 4 + c, :], in_=q_[:, c * 128:(c + 1) * 128], identity=identb[:]),
                                     reads=[q_r, "identb"], writes=["ps_t2"])
                        P.op("dve", lambda e: e.tensor_tensor(out=ckvnT[:, :, j * 128:(j + 1) * 128], in0=ps_t2[:, 0:2, :], in1=gBkv[:], op=ALU.mult),
                             reads=["ps_t2", "gBkv"], writes=["ckvnT"])
                        P.op("dve", lambda e: e.tensor_copy(out=krT[:, :, j * 128:(j + 1) * 128], in_=ps_t2[:, 2:4, :]),
                             reads=["ps_t2"], writes=["krT"])
                        if not own:
                            return
                        P.op("dve", lambda e: e.tensor_tensor(out=cqnT[:, :, t * 128:(t + 1) * 128], in0=ps_t2[:, 4:7, :], in1=gBq[:], op=ALU.mult),
                             reads=["ps_t2", "gBq"], writes=[("cqnT", t)])

                    def S3b(j, t, t0, t1):
                        for c in range(3):
                            P.op("pe", lambda e, c=c: e.matmul(ps_qr[:], lhsT=cqnT[:, c, t * 128:(t + 1) * 128], rhs=w_uq_r[:, c, :],
                                                               start=(c == 0), stop=(c == 2)),
                                 reads=[("cqnT", t), "w_uq_r"], writes=["ps_qr"])
                        for c in range(3):
                            P.op("pe", lambda e, c=c: e.matmul(ps_qrs[:], lhsT=cqnT[:, c, t * 128:(t + 1) * 128], rhs=w_uq_rs[:, c, :],
                                                               start=(c == 0), stop=(c == 2)),
                                 reads=[("cqnT", t), "w_uq_rs"], writes=["ps_qrs"])
                        P.op("dve", lambda e: e.tensor_tensor(out=t0[:].rearrange("p (h r) -> p h r", h=8),
                                                              in0=ps_qr[:].rearrange("p (h r) -> p h r", h=8),
                                                              in1=cos2[:, j, :].unsqueeze(1).to_broadcast([128, 8, 64]), op=ALU.mult),
                             reads=["ps_qr", "cos2"], writes=["rtmp0"])
                        P.op("dve", lambda e: e.tensor_tensor(out=t1[:].rearrange("p (h r) -> p h r", h=8),
                                                              in0=ps_qrs[:].rearrange("p (h r) -> p h r", h=8),
                                                              in1=ssgn[:, j, :].unsqueeze(1).to_broadcast([128, 8, 64]), op=ALU.mult),
                             reads=["ps_qrs", "ssgn"], writes=["rtmp1"])
                        P.op("dve", lambda e: e.tensor_tensor(out=qrr[:], in0=t0[:], in1=t1[:], op=ALU.add),
                             reads=["rtmp0", "rtmp1"], writes=["qrr"])

                    def S3c(j, t):
                        for c in range(4):
                            P.op("pe", lambda e, c=c: e.transpose(out=ps_t3[:, c, :], in_=qrr[:, c * 128:(c + 1) * 128], identity=identb[:]),
                                 reads=["qrr", "identb"], writes=["ps_t3"])
                        P.op("act", lambda e: e.activation(out=qrT[:, :, t * 128:(t + 1) * 128], in_=ps_t3[:, 0:4, :], func=AF.Copy),
                             reads=["ps_t3"], writes=["qrT"])

                    for i in range(NCTX + 2):
                        has3 = i >= 2
                        has2 = 1 <= i <= NCTX
                        if has3:
                            S3(i - 2, 0)
                        if has2:
                            S2(i - 1, 0)
                        if has3:
                            S3(i - 2, 1)
                        if has2:
                            S2(i - 1, 1)
                        if has3:
                            S3(i - 2, 2)
                        if i < NCTX:
                            S1(i)
                    P.barrier()
                with ExitStack() as pb:
                    w_ukv_k = sbuf(pb, "w_ukv_k", [128, 2, 1024], BF16)
                    w_ukv_v = sbuf(pb, "w_ukv_v", [128, 2, 1024], BF16)
                    w_uq_n = sbuf(pb, "w_uq_n", [128, 3, 1024], BF16)
                    KT = [sbuf(pb, "KT", [128, NCTX * 128], BF16) for _ in range(2)]
                    Vh = [sbuf(pb, "Vh", [128, NCTX, 128], BF16) for _ in range(2)]
                    qn = [sbuf(pb, "qn", [128, TOWN], BF16) for _ in range(2)]
                    NPT = 4
                    PT = [sbuf(pb, "PT", [128, 512], BF16) for _ in range(NPT)]
                    rsum = [sbuf(pb, "rsum", [128, 512], F32) for _ in range(2)]
                    PS2 = [sbuf(pb, "PS2", [128, 512], BF16) for _ in range(2)]
                    NPS = 3
                    ps_s = [psum(pb, "ps_s", [128, 512]) for _ in range(NPS)]
                    ps_o = [psum(pb, "ps_o", [128, 512]) for _ in range(2)]
                    ps_m = [psum(pb, "ps_m", [128, 512]) for _ in range(2)]
                    ps_gen = psum(pb, "ps_gen", [128, 512])
                    for dst, src, r in ((w_ukv_k, w_ukv_k_d, "w_ukv_k"), (w_ukv_v, w_ukv_v_d, "w_ukv_v"), (w_uq_n, w_uq_n_d, "w_uq_n")):
                        P.op("pool", lambda e, dst=dst, src=src: e.dma_start(out=dst[:], in_=src), writes=[r], dma=True)

                    for l_ in range(2):
                        for f in range(NF):
                            P.op("pool", lambda e, l_=l_, f=f: e.dma_start(out=wgu_bf[l_, f].rearrange("p a c j -> p a (c j)"),
                                                                            in_=wgu_d[l_, f].rearrange("p a c j -> p a (c j)")),
                                 writes=[("wgu_bf", l_, f)], dma=True)
                        for f2 in range(NF // 2):
                            P.op("pool", lambda e, l_=l_, f2=f2: e.dma_start(out=wd_bf[l_, :, 2 * f2:2 * f2 + 2, :],
                                                                              in_=wd_d[l_, :, 2 * f2:2 * f2 + 2, :]),
                                 writes=[("wd_bf", l_, f2)], dma=True)

                    def gen(h):
                        KT_, KT_r = KT[h % 2], ("KT", h % 2)
                        V_, V_r = Vh[h % 2], ("Vh", h % 2)
                        qn_, qn_r = qn[h % 2], ("qn", h % 2)
                        for g in range(NCTX // 4):
                            for c in range(2):
                                P.op("pe", lambda e, g=g, c=c: e.matmul(ps_gen[:], lhsT=w_ukv_k[:, c, h * 128:(h + 1) * 128],
                                                                        rhs=ckvnT[:, c, g * 512:(g + 1) * 512], start=(c == 0), stop=(c == 1)),
                                     reads=["w_ukv_k", "ckvnT"], writes=["ps_gen"])
                            P.op("dve", lambda e, g=g: e.tensor_copy(out=KT_[:, g * 512:(g + 1) * 512], in_=ps_gen[:]),
                                 reads=["ps_gen"], writes=[KT_r])
                            yield
                        for g in range(NCTX // 4):
                            for jj in range(4):
                                j = g * 4 + jj
                                for c in range(2):
                                    P.op("pe", lambda e, j=j, jj=jj, c=c: e.matmul(
                                        ps_gen[:, jj * 128:(jj + 1) * 128], lhsT=ckvnT[:, c, j * 128:(j + 1) * 128],
                                        rhs=w_ukv_v[:, c, h * 128:(h + 1) * 128], start=(c == 0), stop=(c == 1)),
                                        reads=["w_ukv_v", "ckvnT"], writes=["ps_gen"])
                            P.op("dve", lambda e, g=g: e.tensor_copy(out=V_[:, g * 4:(g + 1) * 4, :],
                                                                     in_=ps_gen[:].rearrange("p (a b) -> p a b", a=4)),
                                 reads=["ps_gen"], writes=[V_r])
                            yield
                        for c0 in range(0, TOWN, 512):
                            n = min(512, TOWN - c0)
                            for c in range(3):
                                P.op("pe", lambda e, c=c, c0=c0, n=n: e.matmul(ps_gen[:, 0:n], lhsT=w_uq_n[:, c, h * 128:(h + 1) * 128],
                                                                               rhs=cqnT[:, c, c0:c0 + n], start=(c == 0), stop=(c == 2)),
                                     reads=["w_uq_n"] + [("cqnT", t) for t in range(NOWN)], writes=["ps_gen"])
                            P.op("dve", lambda e, c0=c0, n=n: e.tensor_copy(out=qn_[:, c0:c0 + n], in_=ps_gen[:, 0:n]),
                                 reads=["ps_gen"], writes=[qn_r])
                            yield

                    steps = []
                    first_of_head = {}
                    gi = 0
                    for h in range(8):
                        first_of_head[len(steps)] = h
                        for sb0 in range(0, NOWN, 4):
                            ntile = min(4, NOWN - sb0)
                            nkb = OWN0 + sb0 + ntile
                            for kb in range(nkb):
                                steps.append(dict(h=h, sb0=sb0, W=ntile * 128, nkb=nkb, kb=kb, g=gi))
                            gi += 1

                    def emit_qk(i):
                        s_ = steps[i]
                        h, kb = s_["h"], s_["kb"]
                        idiag = kb - (OWN0 + s_["sb0"])
                        a0 = max(idiag, 0) * 128
                        n = s_["W"] - a0
                        q0 = s_["sb0"] * 128
                        ps_, ps_r = ps_s[i % NPS], ("ps_s", i % NPS)
                        KT_, qn_ = KT[h % 2], qn[h % 2]
                        P.op("pe", lambda e: e.matmul(ps_[:, 0:n], lhsT=KT_[:, kb * 128:(kb + 1) * 128],
                                                      rhs=qn_[:, q0 + a0:q0 + a0 + n], start=True, stop=False),
                             reads=[("KT", h % 2), ("qn", h % 2)], writes=[ps_r])
                        P.op("pe", lambda e: e.matmul(ps_[:, 0:n], lhsT=krT[:, h % 2, kb * 128:(kb + 1) * 128],
                                                      rhs=qrT[:, h // 2, q0 + a0:q0 + a0 + n], start=False, stop=True),
                             reads=["krT", "qrT"], writes=[ps_r])

                    def emit_rest(i):
                        s_ = steps[i]
                        h, kb, nkb, W, g = s_["h"], s_["kb"], s_["nkb"], s_["W"], s_["g"]
                        idiag = kb - (OWN0 + s_["sb0"])
                        a0 = max(idiag, 0) * 128
                        n = W - a0
                        q0 = s_["sb0"] * 128
                        ps_, ps_r = ps_s[i % NPS], ("ps_s", i % NPS)
                        pt_, pt_r = PT[i % NPT], ("PT", i % NPT)
                        po, po_r = ps_o[g % 2], ("ps_o", g % 2)
                        pm, pm_r = ps_m[g % 2], ("ps_m", g % 2)
                        V_ = Vh[h % 2]
                        P.op("act", lambda e: e.activation(out=pt_[:, 0:n], in_=ps_[:, 0:n], func=AF.Exp, scale=MLA_SCALE, bias=mbt[:, kb:kb + 1]),
                             reads=[ps_r, "mbt"], writes=[pt_r])
                        if idiag >= 0:
                            P.op("dve", lambda e: e.memset(pt_[64:128, 0:64], 0.0), writes=[pt_r])
                        P.op("pe", lambda e: e.matmul(po[:, a0:a0 + n], lhsT=V_[:, kb, :], rhs=pt_[:, 0:n], start=(kb == 0), stop=(kb == nkb - 1)),
                             reads=[("Vh", h % 2), pt_r], writes=[po_r])
                        had_pending = list(pend_sum)
                        del pend_sum[:]
                        for f_ in had_pending:
                            f_()
                        if idiag < 0 and kb % 2 == 0:
                            pass
                        elif idiag < 0:
                            pprev, pprev_r = PT[(i - 1) % NPT], ("PT", (i - 1) % NPT)
                            p2, p2_r = PS2[(i // 2) % 2], ("PS2", (i // 2) % 2)
                            assert steps[i - 1]["g"] == g and steps[i - 1]["kb"] == kb - 1
                            P.op("dve", lambda e: e.tensor_tensor(out=p2[:, 0:W], in0=pprev[:, 0:W], in1=pt_[:, 0:W], op=ALU.add),
                                 reads=[pprev_r, pt_r], writes=[p2_r])
                            pend_sum.append(lambda: P.op("pe", lambda e: e.matmul(pm[:, 0:W], lhsT=onesb[:], rhs=p2[:, 0:W],
                                                                                   start=(kb == 1), stop=False),
                                                         reads=["onesb", p2_r], writes=[pm_r]))
                        else:
                            P.op("pe", lambda e: e.matmul(pm[:, a0:a0 + n], lhsT=onesb[:], rhs=pt_[:, 0:n], start=False, stop=(kb == nkb - 1)),
                                 reads=["onesb", pt_r], writes=[pm_r])
                        if kb == nkb - 1:
                            rs_, rs_r = rsum[g % 2], ("rsum", g % 2)
                            P.op("dve", lambda e: e.reciprocal(out=rs_[:, 0:W], in_=pm[:, 0:W]), reads=[pm_r], writes=[rs_r])
                            P.op("dve", lambda e: e.tensor_tensor(out=oT[:, h, q0:q0 + W], in0=po[:, 0:W], in1=rs_[:, 0:W], op=ALU.mult),
                                 reads=[po_r, rs_r], writes=["oT"])

                    LOOK = 2
                    pend_sum = []
                    for _ in gen(0):
                        pass
                    pending_gen = {}
                    for i0, h in first_of_head.items():
                        if h + 1 < 8:
                            pending_gen[i0 + 8] = h + 1
                    cur_gen = None
                    for i in range(len(steps) + LOOK):
                        if i < len(steps):
                            emit_qk(i)
                        if i in pending_gen:
                            cur_gen = gen(pending_gen[i])
                        if cur_gen is not None and i % 4 == 0:
                            if next(cur_gen, "done") == "done":
                                cur_gen = None
                        if i >= LOOK:
                            emit_rest(i - LOOK)
                    assert cur_gen is None and not pend_sum
                    P.barrier()
            x1 = sbuf(top, "x1", [128, NOWN, D], F32)
            with ExitStack() as pc:
                w_o = sbuf(pc, "w_o", [128, 8, D], BF16, side="right")
                xr = [sbuf(pc, "xr", [128, D], F32) for _ in range(2)]
                ps_y = [psum(pc, "ps_y", [128, D]) for _ in range(2)]
                P.op("pool", lambda e: e.dma_start(out=w_o[:], in_=w_o_d), writes=["w_o"], dma=True)
                load_gT(0)
                for t in range(NOWN):
                    xr_, xr_r = xr[t % 2], ("xr", t % 2)
                    py, py_r = ps_y[t % 2], ("ps_y", t % 2)
                    P.op("sp", lambda e, xr_=xr_, t=t: e.dma_start(out=xr_[:], in_=xin[(OWN0 + t) * 128:(OWN0 + t + 1) * 128, :]),
                         writes=[xr_r], dma=True)
                    for hf in range(2):
                        for h in range(8):
                            P.op("pe", lambda e, py=py, t=t, hf=hf, h=h: e.matmul(
                                py[:, hf * 512:(hf + 1) * 512], lhsT=oT[:, h, t * 128:(t + 1) * 128],
                                rhs=w_o[:, h, hf * 512:(hf + 1) * 512], start=(h == 0), stop=(h == 7)),
                                reads=["oT", "w_o"], writes=[py_r])
                    post_norm_residual(py[:], py_r, x1[:, t, :], [("x1", t)], xr_[:], [xr_r])
                P.barrier()

        def dump_and_finish():
            for t in range(NOWN):
                P.op("sp", lambda e, t=t: e.dma_start(out=out_d[t * 128:(t + 1) * 128, :], in_=x1[:, t, :]),
                     reads=[("x1", t)], writes=[("out", t)], dma=True)

        ps_tr_top = None
        if stop == "C":
            dump_and_finish()
        else:
            with ExitStack() as rest:
                ps_tr_top = psum(rest, "ps_trt", [128, 8, 128], BF16)
                ffn(rest, x1, 0, C_GPRE + 8, 1, ps_tr_top, final_out=False)
                if stop == "D":
                    dump_and_finish()
                else:
                    with ExitStack() as pf:
                        w_kk = sbuf(pf, "w_kk", [128, 8, 1024], BF16)
                        w_kv_ = sbuf(pf, "w_kvv", [128, 8, 1024], BF16)
                        w_q2 = sbuf(pf, "w_q2", [128, 8, 1024], BF16)
                        w_o2 = sbuf(pf, "w_o2", [128, 8, 1024], BF16)
                        NR = 6
                        K2T = sbuf(pf, "K2T", [128, 8, NR * 128], BF16)
                        V2 = sbuf(pf, "V2", [128, NR, 1024], BF16)
                        hb = sbuf(pf, "hb", [128, 8, 256], BF16)
                        oT2 = sbuf(pf, "oT2", [128, 8, 256], BF16)
                        QP = sbuf(pf, "QP", [128, 8, 2, 2, 128], BF16)
                        NPT = 4
                        PT = [sbuf(pf, "PT2", [128, 512], BF16) for _ in range(NPT)]
                        Bt = [sbuf(pf, "Bt", [128, 5, 2, 128], BF16) for _ in range(2)]
                        rsum = sbuf(pf, "rsum2", [128, 512], F32)
                        rstdF = sbuf(pf, "rstdF", [128, NOWN], F32)
                        ps_gy = psum(pf, "ps_gy", [128, D])
                        ps_s2 = psum(pf, "ps_s2", [128, 512])
                        ps_om = psum(pf, "ps_om", [128, 2048])
                        ps_sr = [ps_gy[:, 0:512], ps_gy[:, 512:1024], ps_s2[:]]
                        GY = [("ps_s2", 0), ("ps_s2", 1)]
                        ps_o = [ps_om[:, 0:512], ps_om[:, 512:1024]]
                        ps_m = [ps_om[:, 1024:1536], ps_om[:, 1536:2048]]
                        ybuf = [(ps_gy[:], GY), (ps_om[:, 0:1024], [("ps_o2", 0), ("ps_o2", 1)])]
                        for dst, src, r in ((w_kk, w_kvs_k_d, "w_kk"), (w_kv_, w_kvs_v_d, "w_kvv"), (w_q2, w_q2_d, "w_q2"), (w_o2, w_o2_d, "w_o2")):
                            P.op("pool", lambda e, dst=dst, src=src: e.dma_start(out=dst[:], in_=src), writes=[r], dma=True)
                        for w_, r, col in ((w_kk, "w_kk", C_GPRE + 16), (w_kv_, "w_kvv", C_GPRE + 16), (w_q2, "w_q2", C_GPRE + 24)):
                            for c in range(8):
                                P.op("dve", lambda e, w_=w_, c=c, col=col: e.tensor_scalar(out=w_[:, c, :], in0=w_[:, c, :],
                                                                                           scalar1=consts[:, col + c:col + c + 1], scalar2=None, op0=ALU.mult),
                                     reads=[r, "consts"], writes=[r])
                        load_gT(2)
                        P.op("dve", lambda e: e.memset(QP[:], 0.0), writes=["QP"])
                        for t in range(NOWN):
                            rs, rs_r = rstd_of(x1[:, t, :], [("x1", t)], D, D)
                            P.op("dve", lambda e, t=t, rs=rs: e.tensor_copy(out=rstdF[:, t:t + 1], in_=rs), reads=[rs_r], writes=[("rstdF", t)])
                        gstep = [0]
                        gpair = [0]
                        yk = [0]

                        def stage_a(sb0):
                            for tt in range(2):
                                t = sb0 + tt
                                xs = xs_ring[xs_i[0] % 2]
                                xs_r = ("xs", xs_i[0] % 2)
                                xs_i[0] += 1
                                P.op("act", lambda e, xs=xs, t=t: e.activation(out=xs[:], in_=x1[:, t, :], func=AF.Copy, scale=rstdF[:, t:t + 1]),
                                     reads=[("x1", t), ("rstdF", t)], writes=[xs_r])
                                for c in range(8):
                                    P.op("pe", lambda e, xs=xs, c=c: e.transpose(out=ps_tr_top[:, c, :], in_=xs[:, c * 128:(c + 1) * 128], identity=identb[:]),
                                         reads=[xs_r, "identb"], writes=["ps_tr"])
                                P.op("dve", lambda e, tt=tt: e.tensor_copy(out=hb[:, :, tt * 128:(tt + 1) * 128], in_=ps_tr_top[:]),
                                     reads=["ps_tr"], writes=["hb"])

                        def stage_b(sb0):
                            slot0 = sb0 % NR
                            for pr in range(8):
                                for c in range(8):
                                    P.op("pe", lambda e, pr=pr, c=c: e.matmul(ps_gy[:, 0:256], lhsT=w_kk[:, c, pr * 128:(pr + 1) * 128],
                                                                              rhs=hb[:, c, :], start=(c == 0), stop=(c == 7)),
                                         reads=["w_kk", "hb"], writes=[GY[0]])
                                P.op("dve", lambda e, pr=pr: e.tensor_copy(out=K2T[:, pr, slot0 * 128:(slot0 + 2) * 128], in_=ps_gy[:, 0:256]),
                                     reads=[GY[0]], writes=["K2T"])
                                for c in range(8):
                                    P.op("pe", lambda e, pr=pr, c=c: e.matmul(ps_gy[:, 512:768], lhsT=w_q2[:, c, pr * 128:(pr + 1) * 128],
                                                                              rhs=hb[:, c, :], start=(c == 0), stop=(c == 7)),
                                         reads=["w_q2", "hb"], writes=[GY[1]])
                                for hh in range(2):
                                    P.op("act", lambda e, pr=pr, hh=hh: e.activation(
                                        out=QP[hh * 64:(hh + 1) * 64, pr, :, hh, :],
                                        in_=ps_gy[hh * 64:(hh + 1) * 64, 512:768].rearrange("p (t q) -> p t q", q=128), func=AF.Copy, scale=0.125),
                                        reads=[GY[1]], writes=["QP"])
                            for tt in range(2):
                                for hf in range(2):
                                    for c in range(8):
                                        P.op("pe", lambda e, tt=tt, hf=hf, c=c: e.matmul(
                                            ps_gy[:, hf * 512:(hf + 1) * 512], lhsT=hb[:, c, tt * 128:(tt + 1) * 128],
                                            rhs=w_kv_[:, c, hf * 512:(hf + 1) * 512], start=(c == 0), stop=(c == 7)),
                                            reads=["w_kvv", "hb"], writes=[GY[hf]])
                                P.op("dve", lambda e, tt=tt: e.tensor_copy(out=V2[:, slot0 + tt, :], in_=ps_gy[:]), reads=GY, writes=["V2"])

                        def stage_c(sb0):
                            kb_lo = max(0, sb0 - 4)
                            kbs = list(range(kb_lo, sb0 + 2))
                            steps = [(pr, ki_) for pr in range(8) for ki_ in range(len(kbs))]

                            def geo(kb):
                                js = [j for j in (sb0, sb0 + 1) if 0 <= j - kb <= 4]
                                return (js[0] - sb0), len(js), js[0] - kb

                            def qk(i, gs):
                                pr, ki_ = steps[i]
                                kb = kbs[ki_]
                                t0_, nt, d0 = geo(kb)
                                slot = kb % NR
                                ps_, ps_r = ps_sr[gs % 3], ("ps_s2", gs % 3)
                                b_, b_r = Bt[(gpair[0] + pr) % 2], ("Bt", (gpair[0] + pr) % 2)
                                if ki_ == 0:
                                    P.op("pool", lambda e: e.dma_start(out=b_[:], in_=btab_d[pr]), writes=[b_r], dma=True)
                                ncol = nt * 256
                                P.op("pe", lambda e: e.matmul(ps_[:, 0:ncol], lhsT=K2T[:, pr, slot * 128:(slot + 1) * 128],
                                                              rhs=QP[:, pr, t0_:t0_ + nt, :, :], start=True, stop=False),
                                     reads=["K2T", "QP"], writes=[ps_r])
                                P.op("pe", lambda e: e.matmul(ps_[:, 0:ncol], lhsT=identb[:], rhs=b_[:, d0:d0 + nt, :, :], start=False, stop=True),
                                     reads=["identb", b_r], writes=[ps_r])

                            def rest(i, gs):
                                pr, ki_ = steps[i]
                                kb = kbs[ki_]
                                t0_, nt, d0 = geo(kb)
                                slot = kb % NR
                                ncol = nt * 256
                                ps_, ps_r = ps_sr[gs % 3], ("ps_s2", gs % 3)
                                pt_, pt_r = PT[gs % NPT], ("PT2", gs % NPT)
                                gp = gpair[0] + pr
                                po, po_r = ps_o[gp % 2], ("ps_o2", gp % 2)
                                pm, pm_r = ps_m[gp % 2], ("ps_m2", gp % 2)
                                first, last = (ki_ == 0), (ki_ == len(kbs) - 1)
                                P.op("act", lambda e: e.activation(out=pt_[:, 0:ncol], in_=ps_[:, 0:ncol], func=AF.Exp), reads=[ps_r], writes=[pt_r])
                                ptv = pt_[:, 0:ncol].rearrange("p (t h q) -> p t h q", h=2, q=128)
                                for hh in range(2):
                                    h = 2 * pr + hh
                                    P.op("pe", lambda e, hh=hh, h=h: e.matmul(
                                        po[hh * 64:(hh + 1) * 64, t0_ * 128:(t0_ + nt) * 128].rearrange("p (t q) -> p t q", q=128),
                                        lhsT=V2[:, slot, h * 64:(h + 1) * 64], rhs=ptv[:, :, hh, :], start=first, stop=last, skip_group_check=True),
                                        reads=["V2", pt_r], writes=[po_r])
                                P.op("pe", lambda e: e.matmul(pm[:, t0_ * 256:t0_ * 256 + ncol], lhsT=onesb[:], rhs=pt_[:, 0:ncol], start=first, stop=last),
                                     reads=["onesb", pt_r], writes=[pm_r])
                                if last:
                                    P.op("dve", lambda e: e.reciprocal(out=rsum[:], in_=pm), reads=[pm_r], writes=["rsum2"])
                                    rsv = rsum[:].rearrange("p (t h q) -> p t h q", h=2, q=128)
                                    for hh in range(2):
                                        P.op("dve", lambda e, hh=hh: e.tensor_tensor(
                                            out=oT2[hh * 64:(hh + 1) * 64, pr, :].rearrange("p (t q) -> p t q", q=128),
                                            in0=po[hh * 64:(hh + 1) * 64, 0:256].rearrange("p (t q) -> p t q", q=128),
                                            in1=rsv[hh * 64:(hh + 1) * 64, :, hh, :], op=ALU.mult),
                                            reads=[po_r, "rsum2"], writes=["oT2"])

                            LOOK = 2
                            for i in range(len(steps) + LOOK):
                                if i < len(steps):
                                    qk(i, gstep[0] + i)
                                if i >= LOOK:
                                    rest(i - LOOK, gstep[0] + i - LOOK)
                            gstep[0] += len(steps)
                            gpair[0] += 8

                        def stage_d(sb0, tt):
                            py, py_r = ybuf[yk[0] % 2]
                            yk[0] += 1
                            for hf in range(2):
                                for pr in range(8):
                                    P.op("pe", lambda e, py=py, tt=tt, hf=hf, pr=pr: e.matmul(
                                        py[:, hf * 512:(hf + 1) * 512], lhsT=oT2[:, pr, tt * 128:(tt + 1) * 128],
                                        rhs=w_o2[:, pr, hf * 512:(hf + 1) * 512], start=(pr == 0), stop=(pr == 7)),
                                        reads=["oT2", "w_o2"], writes=py_r)
                            t = sb0 + tt
                            post_norm_residual(py, py_r, x1[:, t, :], [("x1", t)], x1[:, t, :], [("x1", t)])

                        stage_a(0)
                        stage_b(0)
                        for sb0 in range(0, NOWN, 2):
                            stage_c(sb0)
                            nxt = sb0 + 2 < NOWN
                            if nxt:
                                stage_a(sb0 + 2)
                            stage_d(sb0, 0)
                            stage_d(sb0, 1)
                            if nxt:
                                stage_b(sb0 + 2)
                        P.barrier()
                    if stop == "F":
                        dump_and_finish()
                    else:
                        ffn(rest, x1, 1, C_GPRE + 32, 3, ps_tr_top, final_out=True)
        P.op("sp", lambda e: None, reads=[("out", t) for t in range(NOWN)])
        P.emit(top)
    return nc


def _kc(w):
    k, n = w.shape
    return np.ascontiguousarray(w.reshape(k // 128, 128, n).transpose(1, 0, 2))


def _prep(inputs):
    f = lambda a: np.asarray(a, dtype=np.float32)
    x = f(inputs["x"])
    positions = np.asarray(inputs["positions"]).astype(np.int32)
    w_a = f(inputs["mla_w_a"])[0]
    w_uq = f(inputs["mla_w_uq"])[0]
    w_ukv = f(inputs["mla_w_ukv"])[0]
    sw = (np.arange(64) + 32) % 64
    shared = {}
    shared["w_a_kv"] = _kc(np.concatenate([w_a[:, 384:640], w_a[:, 640:704], w_a[:, 640:704][:, sw]], axis=1))
    shared["w_a_q"] = _kc(w_a[:, 0:384])
    uq = w_uq.reshape(384, 8, 192)
    shared["w_uq_n"] = _kc(np.ascontiguousarray(uq[:, :, 0:128]).reshape(384, 1024))
    shared["w_uq_r"] = _kc(np.ascontiguousarray(uq[:, :, 128:192]).reshape(384, 512))
    shared["w_uq_rs"] = _kc(np.ascontiguousarray(uq[:, :, 128:192][:, :, sw]).reshape(384, 512))
    ukv = w_ukv.reshape(256, 8, 256)
    shared["w_ukv_k"] = _kc(np.ascontiguousarray(ukv[:, :, 0:128]).reshape(256, 1024))
    shared["w_ukv_v"] = _kc(np.ascontiguousarray(ukv[:, :, 128:256]).reshape(256, 1024))
    shared["w_o"] = _kc(f(inputs["mla_w_o"])[0])
    wg = f(inputs["ffn_w_gate"]).reshape(2, 8, 128, NF, 128)
    wu = f(inputs["ffn_w_up"]).reshape(2, 8, 128, NF, 128)
    wgu = np.stack([wg, wu], axis=1)
    shared["wgu"] = np.ascontiguousarray(wgu.transpose(0, 4, 3, 1, 2, 5))
    shared["wd"] = np.ascontiguousarray(f(inputs["ffn_w_down"]).reshape(2, NF, 128, D).transpose(0, 2, 1, 3))
    wkv = f(inputs["w_kv_shared"])
    shared["w_kvs_k"] = _kc(wkv[:, 0:1024])
    shared["w_kvs_v"] = _kc(wkv[:, 1024:2048])
    shared["w_q2"] = _kc(f(inputs["b_w_q"])[0])
    shared["w_o2"] = _kc(f(inputs["b_w_o"])[0])
    rel = f(inputs["b_rel_table"])[0]
    kl = np.arange(128)[:, None]
    cc = np.arange(640)[None, :]
    idx = np.clip(cc - kl, -63, 256) + 63
    bt = rel[:, idx]
    dl = cc // 128
    qh = (cc % 128) >= 64
    kh = kl >= 64
    invalid = ((dl == 0) & (~qh) & kh) | ((dl == 4) & qh & (~kh))
    bt = np.where(invalid[None], np.float32(NEG), bt).astype(np.float32)
    shared["btab"] = np.ascontiguousarray(bt.reshape(8, 2, 128, 5, 128).transpose(0, 2, 3, 1, 4))
    consts = np.zeros((128, NCONST), np.float32)
    pre = [f(inputs["attn_pre_g"])[0], f(inputs["ffn_pre_g"])[0], f(inputs["kv_src_g"]),
           f(inputs["attn_pre_g"])[1], f(inputs["ffn_pre_g"])[1]]
    for i, g in enumerate(pre):
        consts[:, C_GPRE + 8 * i:C_GPRE + 8 * (i + 1)] = g.reshape(8, 128).T
    consts[:, C_GQ:C_GQ + 3] = f(inputs["mla_g_q"])[0].reshape(3, 128).T
    consts[:, C_GKV:C_GKV + 2] = f(inputs["mla_g_kv"])[0].reshape(2, 128).T
    inv = (1.0 / (np.float32(10000.0) ** (np.arange(0, 64, 2, dtype=np.float32) / np.float32(64)))).astype(np.float32)
    consts[:, C_INV:C_INV + 32] = (inv.astype(np.float64) / (2 * np.pi)).astype(np.float32)[None, :]
    consts[:, C_IDENT:C_IDENT + 128] = np.eye(128, dtype=np.float32)
    shared["consts"] = consts
    shared["gpost"] = np.stack([f(inputs["attn_post_g"])[0], f(inputs["ffn_post_g"])[0],
                                f(inputs["attn_post_g"])[1], f(inputs["ffn_post_g"])[1]]).astype(np.float32)
    in_maps = []
    for core in range(8):
        b, half = core // 2, core % 2
        m = dict(shared)
        if half == 0:
            xin = np.concatenate([np.zeros((OWN0 * 128, D), np.float32), x[b, 0:TOWN]], axis=0)
            pos = np.concatenate([np.zeros(OWN0 * 128, np.int32), positions[b, 0:TOWN]])
            mb = np.zeros((128, NCTX), np.float32)
            mb[:, 0:OWN0] = NEG
        else:
            xin = x[b]
            pos = positions[b]
            mb = np.zeros((128, NCTX), np.float32)
        m["xin"] = np.ascontiguousarray(xin)
        m["posT"] = np.ascontiguousarray(pos.reshape(NCTX, 128).T.astype(np.int32))
        m["mb"] = mb
        in_maps.append(m)
    return in_maps


_NC_CACHE = {}


def _run(inputs, stop="G"):
    if stop not in _NC_CACHE:
        _NC_CACHE[stop] = build(stop)
    nc = _NC_CACHE[stop]
    in_maps = _prep(inputs)
    res = run_bass_kernel_spmd(nc, in_maps, core_ids=list(range(8)))
    out = np.zeros((4, 4096, D), np.float32)
    for core in range(8):
        b, half = core // 2, core % 2
        o = np.asarray(res.results[core]["out"], dtype=np.float32)
        if half == 0:
            out[b, 0:TOWN] = o
        else:
            out[b, TOWN:4096] = o[TOWN - (4096 - TOWN):]
    return out


def kernel(**inputs):
    return _run(inputs, "G")
```

```python
import numpy as np
from contextlib import ExitStack
import concourse.bass as bass
import concourse.mybir as mybir
from concourse.bass_utils import run_bass_kernel_spmd

F32 = mybir.dt.float32
BF16 = mybir.dt.bfloat16
I32 = mybir.dt.int32
AF = mybir.ActivationFunctionType
ALU = mybir.AluOpType

D = 1024
NCTX = 32
NOWN = 18
OWN0 = NCTX - NOWN
TOWN = NOWN * 128
DFF = 2816
NF = DFF // 128
EPS = 1e-6
NEG = -30000.0
TWO_PI = float(2 * np.pi * (1 - 1e-6))
MLA_SCALE = float(192 ** -0.5)
ENGS = ("pe", "act", "dve", "pool", "sp")


class Op:
    __slots__ = ("eng", "fn", "deps", "dma", "sig", "sem", "semval", "idx", "prewait")

    def __init__(self, eng, fn, dma):
        self.eng = eng
        self.fn = fn
        self.dma = dma
        self.deps = set()
        self.sig = False
        self.sem = None
        self.semval = 0
        self.prewait = None


class Prog:
    def __init__(self, nc, n_dma_sems=(24, 12)):
        self.nc = nc
        self.ops = []
        self.last_w = {}
        self.readers = {}
        self.n_dma_sems = {"sp": n_dma_sems[0], "pool": n_dma_sems[1]}
        self.dma_since_bar = []
        self.nbar = 0

    def op(self, eng, fn, reads=(), writes=(), dma=False):
        o = Op(eng, fn, dma)
        o.idx = len(self.ops)
        key = ("dma", o.idx) if dma else eng
        for r in reads:
            o.deps.update(self.last_w.get(r, {}).values())
            if self._is_psum(r):
                o.deps.update(i for k, i in self.readers.get(r, {}).items() if k != key)
        for w_ in writes:
            o.deps.update(self.last_w.get(w_, {}).values())
            o.deps.update(self.readers.get(w_, {}).values())
        for r in reads:
            self._put(self.readers.setdefault(r, {}), key, o.idx)
        for w_ in writes:
            self._put(self.last_w.setdefault(w_, {}), key, o.idx)
        o.deps.discard(o.idx)
        self.ops.append(o)
        if dma:
            self.dma_since_bar.append(o.idx)
        return o

    @staticmethod
    def _is_psum(r):
        n = r[0] if isinstance(r, tuple) else r
        return isinstance(n, str) and n.startswith("ps")

    @staticmethod
    def _put(d, key, idx):
        d[key] = idx
        if len(d) > 24:
            for k in sorted((k for k in d if isinstance(k, tuple)), key=lambda k: k[1])[:8]:
                del d[k]

    def barrier(self):
        n = self.nbar
        self.nbar += 1
        sig = []
        for e in ("pe", "act", "dve", "pool"):
            sig.append(self.op(e, lambda eng: eng.drain(), writes=[("bar", n, e)]).idx)
        extra = set(sig) | set(self.dma_since_bar)
        self.dma_since_bar = []
        for e in ENGS:
            o = self.op(e, lambda eng: None)
            o.deps |= extra

    def emit(self, stack):
        nc = self.nc
        ops = self.ops
        for o in ops:
            if o.eng == "pe" and not o.dma:
                o.deps = {d for d in o.deps if not (ops[d].eng == "pe" and not ops[d].dma)}
        for o in ops:
            for d in o.deps:
                ops[d].sig = True
        sems = {e: stack.enter_context(nc.semaphore("c_" + e)) for e in ("pe", "act", "dve", "pool")}
        dsems = {q: [stack.enter_context(nc.semaphore("d_%s%d" % (q, i))) for i in range(n)]
                 for q, n in self.n_dma_sems.items()}
        cnt = {e: 0 for e in ENGS}
        dcnt = {q: 0 for q in dsems}
        duse = {q: [0] * len(dsems[q]) for q in dsems}
        for o in ops:
            if o.dma:
                q = o.eng
                i = dcnt[q] % len(dsems[q])
                dcnt[q] += 1
                o.sem = dsems[q][i]
                if duse[q][i] > 0:
                    o.prewait = (o.sem, 16 * duse[q][i])
                duse[q][i] += 1
                o.semval = 16 * duse[q][i]
            elif o.sig:
                cnt[o.eng] += 1
                o.sem = sems[o.eng]
                o.semval = cnt[o.eng]
        per = {e: [] for e in ENGS}
        for o in ops:
            per[o.eng].append(o)

        def run(ename, eng):
            waited = {}
            for o in per[ename]:
                need = {}
                if o.prewait is not None:
                    need[id(o.prewait[0])] = o.prewait
                for d in o.deps:
                    p = ops[d]
                    k = id(p.sem)
                    if k not in need or need[k][1] < p.semval:
                        need[k] = (p.sem, p.semval)
                for k, (s, v) in need.items():
                    if waited.get(k, 0) >= v:
                        continue
                    eng.wait_ge(s, v)
                    waited[k] = v
                ins = o.fn(eng)
                if ins is None:
                    continue
                if o.dma:
                    ins.then_inc(o.sem, 16)
                elif o.sig:
                    ins.then_inc(o.sem, 1)

        block = stack.enter_context(nc.Block())

        @block.tensor
        def _(e):
            run("pe", e)

        @block.scalar
        def _(e):
            run("act", e)

        @block.vector
        def _(e):
            run("dve", e)

        @block.gpsimd
        def _(e):
            run("pool", e)

        @block.sync
        def _(e):
            run("sp", e)


C_GPRE = 0
C_GQ = 40
C_GKV = 43
C_INV = 45
C_IDENT = 77
NCONST = C_IDENT + 128

STOP_PHASES = ("C", "D", "F", "G")


def build(stop="G"):
    nc = bass.Bass("TRN2", target_bir_lowering=False)

    def din(name, shape, dt=F32):
        return nc.dram_tensor(name, list(shape), dt, kind="ExternalInput").ap()

    xin = din("xin", [NCTX * 128, D])
    posT = din("posT", [128, NCTX], I32)
    mb_d = din("mb", [128, NCTX])
    consts_d = din("consts", [128, NCONST])
    gpost_d = din("gpost", [4, D])
    w_a_kv_d = din("w_a_kv", [128, 8, 384])
    w_a_q_d = din("w_a_q", [128, 8, 384])
    w_uq_n_d = din("w_uq_n", [128, 3, 1024])
    w_uq_r_d = din("w_uq_r", [128, 3, 512])
    w_uq_rs_d = din("w_uq_rs", [128, 3, 512])
    w_ukv_k_d = din("w_ukv_k", [128, 2, 1024])
    w_ukv_v_d = din("w_ukv_v", [128, 2, 1024])
    w_o_d = din("w_o", [128, 8, 1024])
    wgu_d = din("wgu", [2, NF, 128, 2, 8, 128])
    wd_d = din("wd", [2, 128, NF, 1024])
    w_kvs_k_d = din("w_kvs_k", [128, 8, 1024])
    w_kvs_v_d = din("w_kvs_v", [128, 8, 1024])
    w_q2_d = din("w_q2", [128, 8, 1024])
    w_o2_d = din("w_o2", [128, 8, 1024])
    btab_d = din("btab", [8, 128, 5, 2, 128])
    out_d = nc.dram_tensor("out", [TOWN, D], F32, kind="ExternalOutput").ap()
    wgu_bf = nc.dram_tensor("wgu_bf", [2, NF, 128, 2, 8, 128], BF16).ap()
    wd_bf = nc.dram_tensor("wd_bf", [2, 128, NF, 1024], BF16).ap()

    P = Prog(nc)
    uid = [0]

    def nm(s):
        uid[0] += 1
        return "%s_%d" % (s, uid[0])

    with ExitStack() as top:
        def sbuf(st, name, shape, dt, side=None):
            return st.enter_context(nc.sbuf_tensor(nm(name), list(shape), dt, side=side))

        def psum(st, name, shape, dt=F32):
            return st.enter_context(nc.psum_tensor(nm(name), list(shape), dt))

        consts = sbuf(top, "consts", [128, NCONST], F32)
        identb = sbuf(top, "identb", [128, 128], BF16)
        onesb = sbuf(top, "onesb", [128, 128], BF16)
        mbt = sbuf(top, "mbt", [128, NCTX], F32)
        mhalf = sbuf(top, "mhalf", [128, 1], F32)
        stat = sbuf(top, "stat", [128, 64], F32)
        gT = sbuf(top, "gT", [128, D], F32)
        tmpA = sbuf(top, "tmpA", [128, D], F32)
        junk = sbuf(top, "junk", [128, D], BF16)
        xs_ring = [sbuf(top, "xs", [128, D], BF16) for _ in range(2)]
        gB = sbuf(top, "gB", [128, 8, 128], BF16)

        P.op("sp", lambda e: e.dma_start(out=consts[:], in_=consts_d), writes=["consts"], dma=True)
        P.op("sp", lambda e: e.dma_start(out=mbt[:], in_=mb_d), writes=["mbt"], dma=True)
        P.op("dve", lambda e: e.tensor_copy(out=identb[:], in_=consts[:, C_IDENT:C_IDENT + 128]),
             reads=["consts"], writes=["identb"])
        P.op("dve", lambda e: e.memset(onesb[:], 1.0), writes=["onesb"])
        P.op("dve", lambda e: e.memset(mhalf[:], -0.5), writes=["mhalf"])

        stat_i = [0]

        def stat_col():
            i = stat_i[0] % 64
            stat_i[0] += 1
            return stat[:, i:i + 1], ("stat", i)

        def load_gB(col0, nch=8):
            for c in range(nch):
                P.op("dve", lambda e, c=c: e.tensor_scalar(out=gB[:, c, :], in0=onesb[:], scalar1=consts[:, col0 + c:col0 + c + 1],
                                                           scalar2=None, op0=ALU.mult),
                     reads=["onesb", "consts"], writes=["gB"])

        def load_gT(row):
            P.op("sp", lambda e: e.dma_start(out=gT[:], in_=gpost_d[row].partition_broadcast(128)), writes=["gT"], dma=True)

        def rstd_of(src_ap, src_res, width, dim):
            ss, ss_r = stat_col()
            rs, rs_r = stat_col()
            P.op("act", lambda e: e.activation(out=junk[:, 0:width], in_=src_ap, func=AF.Square, accum_out=ss),
                 reads=list(src_res), writes=["junk", ss_r])
            P.op("dve", lambda e: e.tensor_scalar(out=ss, in0=ss, scalar1=1.0 / dim, scalar2=EPS, op0=ALU.mult, op1=ALU.add),
                 reads=[ss_r], writes=[ss_r])
            P.op("pool", lambda e: e.tensor_tensor(out=rs, in0=ss, in1=mhalf[:], op=ALU.pow), reads=[ss_r, "mhalf"], writes=[rs_r])
            return rs, rs_r

        xs_i = [0]

        def norm_copy(src_ap, src_res, rs, rs_r, nch=8):
            xs = xs_ring[xs_i[0] % 2]
            xs_r = ("xs", xs_i[0] % 2)
            xs_i[0] += 1
            P.op("act", lambda e: e.activation(out=xs[:, 0:nch * 128], in_=src_ap, func=AF.Copy, scale=rs),
                 reads=list(src_res) + [rs_r], writes=[xs_r])
            return xs, xs_r

        def norm_apply(src_ap, src_res, rs, rs_r, dst_ap, dst_res, ps_tr, ps_res, nch=8, pre=None):
            xs, xs_r = pre if pre is not None else norm_copy(src_ap, src_res, rs, rs_r, nch)
            for c in range(nch):
                P.op("pe", lambda e, c=c: e.transpose(out=ps_tr[:, c, :], in_=xs[:, c * 128:(c + 1) * 128], identity=identb[:]),
                     reads=[xs_r, "identb"], writes=[ps_res])
            P.op("dve", lambda e: e.tensor_tensor(out=dst_ap, in0=ps_tr[:, 0:nch, :], in1=gB[:, 0:nch, :], op=ALU.mult),
                 reads=[ps_res, "gB"], writes=list(dst_res))

        def norm_T(src_ap, src_res, dst_ap, dst_res, ps_tr, ps_res, nch=8):
            rs, rs_r = rstd_of(src_ap, src_res, nch * 128, nch * 128)
            norm_apply(src_ap, src_res, rs, rs_r, dst_ap, dst_res, ps_tr, ps_res, nch)

        def post_norm_residual(ps_y, ps_res, x_dst, x_dst_res, x_src, x_src_res):
            ps_rl = list(ps_res) if isinstance(ps_res, list) else [ps_res]
            rs, rs_r = rstd_of(ps_y, ps_rl, D, D)
            P.op("dve", lambda e: e.scalar_tensor_tensor(out=tmpA[:], in0=ps_y, scalar=rs, in1=gT[:], op0=ALU.mult, op1=ALU.mult),
                 reads=ps_rl + [rs_r, "gT"], writes=["tmpA"])
            P.op("dve", lambda e: e.tensor_tensor(out=x_dst, in0=tmpA[:], in1=x_src, op=ALU.add),
                 reads=["tmpA"] + list(x_src_res), writes=list(x_dst_res))

        def ffn(st_outer, x1, layer, gpre_col, gpost_row, ps_tr, final_out):
            GT = 6
            G = GT * 128
            NG = NOWN // GT
            with ExitStack() as st:
                wd = sbuf(st, "wd", [128, NF, D], BF16)
                actT = sbuf(st, "actT", [128, NF, G], BF16)
                hT = sbuf(st, "hT", [128, 8, G], BF16)
                wgu = [sbuf(st, "wgu", [128, 2, 8, 128], BF16) for _ in range(4)]
                sg = [sbuf(st, "sg", [128, 512], BF16) for _ in range(2)]
                pA = psum(st, "pA", [128, D])
                pB = psum(st, "pB", [128, D])
                pC = psum(st, "pC", [128, D])
                ps_g = [pA[:, 0:512], pB[:, 0:512]]
                ps_u = [pA[:, 512:1024], pB[:, 512:1024]]
                ps_y = [pC, pA]
                ps_y_r = [[("ps_y", 0)], [("ps_g", 0), ("ps_u", 0)]]
                for half in range(2):
                    P.op("sp", lambda e, half=half: e.dma_start(out=wd[:, half * 11:(half + 1) * 11, :],
                                                                 in_=wd_bf[layer, :, half * 11:(half + 1) * 11, :]),
                         reads=[("wd_bf", layer, f2) for f2 in range(NF // 2)], writes=[("wd", half)], dma=True)
                load_gB(gpre_col)
                load_gT(gpost_row)
                kk = [0]
                yk = [0]

                def d1_stats(g0):
                    return [rstd_of(x1[:, g0 + t, :], [("x1", g0 + t)], D, D) for t in range(GT)]

                def d1_copy(g0, t, rr):
                    return norm_copy(x1[:, g0 + t, :], [("x1", g0 + t)], rr[0], rr[1])

                def d1_apply(g0, t, rr, pre=None):
                    norm_apply(x1[:, g0 + t, :], [("x1", g0 + t)], rr[0], rr[1], hT[:, :, t * 128:(t + 1) * 128], ["hT"], ps_tr, "ps_tr", pre=pre)

                def d2(g0):
                    for f in range(NF):
                        k = kk[0]
                        w = wgu[k % 4]
                        w_r = ("wgu", k % 4)
                        P.op("sp", lambda e, w=w, f=f: e.dma_start(out=w[:], in_=wgu_bf[layer, f]),
                             reads=[("wgu_bf", layer, f)], writes=[w_r], dma=True)
                        for pi, (c0, c1) in enumerate(((0, 512), (512, G))):
                            j = 2 * k + pi
                            pg, pu = ps_g[j % 2], ps_u[j % 2]
                            pg_r, pu_r = ("ps_g", j % 2), ("ps_u", j % 2)
                            s_, s_r = sg[j % 2], ("sg", j % 2)
                            n = c1 - c0
                            for c in range(8):
                                P.op("pe", lambda e, pg=pg, w=w, c=c, c0=c0, c1=c1, n=n: e.matmul(
                                    pg[:, 0:n], lhsT=w[:, 0, c, :], rhs=hT[:, c, c0:c1], start=(c == 0), stop=(c == 7)),
                                    reads=[w_r, "hT"], writes=[pg_r])
                            for c in range(8):
                                P.op("pe", lambda e, pu=pu, w=w, c=c, c0=c0, c1=c1, n=n: e.matmul(
                                    pu[:, 0:n], lhsT=w[:, 1, c, :], rhs=hT[:, c, c0:c1], start=(c == 0), stop=(c == 7)),
                                    reads=[w_r, "hT"], writes=[pu_r])
                            P.op("act", lambda e, pg=pg, s_=s_, n=n: e.activation(out=s_[:, 0:n], in_=pg[:, 0:n], func=AF.Silu),
                                 reads=[pg_r], writes=[s_r])
                            P.op("dve", lambda e, pu=pu, s_=s_, f=f, c0=c0, c1=c1, n=n: e.tensor_tensor(
                                out=actT[:, f, c0:c1], in0=pu[:, 0:n], in1=s_[:, 0:n], op=ALU.mult),
                                reads=[pu_r, s_r], writes=["actT"])
                        kk[0] += 1

                def d3_tile(g0, t):
                    py = ps_y[yk[0] % 2]
                    py_r = ps_y_r[yk[0] % 2]
                    yk[0] += 1
                    for hf in range(2):
                        for f in range(NF):
                            P.op("pe", lambda e, py=py, t=t, hf=hf, f=f: e.matmul(
                                py[:, hf * 512:(hf + 1) * 512], lhsT=actT[:, f, t * 128:(t + 1) * 128],
                                rhs=wd[:, f, hf * 512:(hf + 1) * 512], start=(f == 0), stop=(f == NF - 1)),
                                reads=["actT", ("wd", f // 11)], writes=py_r)
                    tt = g0 + t
                    post_norm_residual(py[:], py_r, x1[:, tt, :], [("x1", tt)], x1[:, tt, :], [("x1", tt)])
                    if final_out:
                        P.op("sp", lambda e, tt=tt: e.dma_start(out=out_d[tt * 128:(tt + 1) * 128, :], in_=x1[:, tt, :]),
                             reads=[("x1", tt)], writes=[("out", tt)], dma=True)

                rr = d1_stats(0)
                for t in range(GT):
                    d1_apply(0, t, rr[t])
                for gi_ in range(NG):
                    g0 = gi_ * GT
                    d2(g0)
                    pre = None
                    if gi_ + 1 < NG:
                        rr = d1_stats(g0 + GT)
                        pre = d1_copy(g0 + GT, 0, rr[0])
                    for t in range(GT):
                        d3_tile(g0, t)
                        if gi_ + 1 < NG:
                            d1_apply(g0 + GT, t, rr[t], pre=pre)
                            pre = d1_copy(g0 + GT, t + 1, rr[t + 1]) if t + 1 < GT else None
            P.barrier()

        with ExitStack() as mla:
            oT = sbuf(mla, "oT", [128, 8, TOWN], BF16, side="right")
            with ExitStack() as mla_ab:
                ckvnT = sbuf(mla_ab, "ckvnT", [128, 2, NCTX * 128], BF16, side="right")
                krT = sbuf(mla_ab, "krT", [128, 2, NCTX * 128], BF16, side="right")
                cqnT = sbuf(mla_ab, "cqnT", [128, 3, TOWN], BF16, side="right")
                qrT = sbuf(mla_ab, "qrT", [128, 4, TOWN], BF16, side="right")
                with ExitStack() as pa:
                    cos2 = sbuf(pa, "cos2", [128, NCTX, 64], F32)
                    ssgn = sbuf(pa, "ssgn", [128, NCTX, 64], F32)
                    w_a_kv = sbuf(pa, "w_a_kv", [128, 8, 384], BF16)
                    w_a_q = sbuf(pa, "w_a_q", [128, 8, 384], BF16)
                    w_uq_r = sbuf(pa, "w_uq_r", [128, 3, 512], BF16)
                    w_uq_rs = sbuf(pa, "w_uq_rs", [128, 3, 512], BF16)
                    xt = [sbuf(pa, "xt", [128, D], F32) for _ in range(3)]
                    hTt = [sbuf(pa, "hTt", [128, 8, 128], BF16) for _ in range(2)]
                    akv_s = [sbuf(pa, "akv_s", [128, 512], BF16) for _ in range(2)]
                    aq_s = [sbuf(pa, "aq_s", [128, 384], BF16) for _ in range(2)]
                    gBkv = sbuf(pa, "gBkv", [128, 2, 128], BF16)
                    gBq = sbuf(pa, "gBq", [128, 3, 128], BF16)
                    rtmp = [sbuf(pa, "rtmp", [128, 512], F32) for _ in range(2)]
                    qrr = sbuf(pa, "qrr", [128, 512], BF16)
                    ps_tr = psum(pa, "ps_tr", [128, 8, 128], BF16)
                    ps_akv = [psum(pa, "ps_akv", [128, 512]) for _ in range(2)]
                    ps_aq = psum(pa, "ps_aq", [128, 512])
                    ps_t2 = psum(pa, "ps_t2", [128, 8, 128], BF16)
                    ps_qr = psum(pa, "ps_qr", [128, 512])
                    ps_qrs = psum(pa, "ps_qrs", [128, 512])
                    ps_t3 = psum(pa, "ps_t3", [128, 8, 128], BF16)

                    for dst, src, r in ((w_a_kv, w_a_kv_d, "w_a_kv"), (w_a_q, w_a_q_d, "w_a_q"),
                                        (w_uq_r, w_uq_r_d, "w_uq_r"), (w_uq_rs, w_uq_rs_d, "w_uq_rs")):
                        P.op("pool", lambda e, dst=dst, src=src: e.dma_start(out=dst[:], in_=src), writes=[r], dma=True)
                    load_gB(C_GPRE + 0)
                    for i_ in range(2):
                        P.op("dve", lambda e, i_=i_: e.memset(akv_s[i_][:, 320:448], 0.0), writes=[("akv_s", i_)])
                    for c in range(2):
                        P.op("dve", lambda e, c=c: e.tensor_scalar(out=gBkv[:, c, :], in0=onesb[:], scalar1=consts[:, C_GKV + c:C_GKV + c + 1],
                                                                   scalar2=None, op0=ALU.mult), reads=["onesb", "consts"], writes=["gBkv"])
                    for c in range(3):
                        P.op("dve", lambda e, c=c: e.tensor_scalar(out=gBq[:, c, :], in0=onesb[:], scalar1=consts[:, C_GQ + c:C_GQ + c + 1],
                                                                   scalar2=None, op0=ALU.mult), reads=["onesb", "consts"], writes=["gBq"])
                    with ExitStack() as rp:
                        posi = sbuf(rp, "posi", [128, NCTX], I32)
                        posf = sbuf(rp, "posf", [128, NCTX], F32)
                        u = sbuf(rp, "u", [128, NCTX, 32], F32)
                        ki = sbuf(rp, "ki", [128, NCTX, 32], I32)
                        kf = sbuf(rp, "kf", [128, NCTX, 32], F32)
                        fw = sbuf(rp, "fw", [128, NCTX, 32], F32)
                        P.op("sp", lambda e: e.dma_start(out=posi[:], in_=posT), writes=["posi"], dma=True)
                        P.op("dve", lambda e: e.tensor_copy(out=posf[:], in_=posi[:]), reads=["posi"], writes=["posf"])
                        P.op("dve", lambda e: e.tensor_tensor(out=u[:], in0=posf[:].unsqueeze(2).to_broadcast([128, NCTX, 32]),
                                                              in1=consts[:, C_INV:C_INV + 32].unsqueeze(1).to_broadcast([128, NCTX, 32]),
                                                              op=ALU.mult), reads=["posf", "consts"], writes=["u"])
                        P.op("dve", lambda e: e.tensor_copy(out=ki[:], in_=u[:]), reads=["u"], writes=["ki"])
                        P.op("dve", lambda e: e.tensor_copy(out=kf[:], in_=ki[:]), reads=["ki"], writes=["kf"])
                        P.op("dve", lambda e: e.tensor_tensor(out=u[:], in0=u[:], in1=kf[:], op=ALU.subtract), reads=["u", "kf"], writes=["u"])

                        def wrapped_sin(shift, scale, dst, dst_r):
                            P.op("dve", lambda e: e.tensor_scalar(out=fw[:], in0=u[:], scalar1=shift, scalar2=None, op0=ALU.add), reads=["u"], writes=["fw"])
                            P.op("dve", lambda e: e.tensor_scalar(out=kf[:], in0=fw[:], scalar1=0.5, scalar2=None, op0=ALU.is_gt), reads=["fw"], writes=["kf"])
                            P.op("dve", lambda e: e.tensor_tensor(out=fw[:], in0=fw[:], in1=kf[:], op=ALU.subtract), reads=["fw", "kf"], writes=["fw"])
                            P.op("dve", lambda e: e.tensor_scalar(out=kf[:], in0=fw[:], scalar1=-0.5, scalar2=None, op0=ALU.is_lt), reads=["fw"], writes=["kf"])
                            P.op("dve", lambda e: e.tensor_tensor(out=fw[:], in0=fw[:], in1=kf[:], op=ALU.add), reads=["fw", "kf"], writes=["fw"])
                            P.op("act", lambda e: e.activation(out=dst, in_=fw[:], func=AF.Sin, scale=scale), reads=["fw"], writes=[dst_r])

                        wrapped_sin(0.25, TWO_PI, cos2[:, :, 0:32], "cos2")
                        P.op("dve", lambda e: e.tensor_copy(out=cos2[:, :, 32:64], in_=cos2[:, :, 0:32]), reads=["cos2"], writes=["cos2"])
                        wrapped_sin(0.0, TWO_PI, ssgn[:, :, 32:64], "ssgn")
                        P.op("dve", lambda e: e.tensor_scalar(out=ssgn[:, :, 0:32], in0=ssgn[:, :, 32:64], scalar1=-1.0, scalar2=None, op0=ALU.mult),
                             reads=["ssgn"], writes=["ssgn"])
                        P.barrier()

                    rsx = {}
                    rskv = {}
                    rsq = {}

                    def S1(j):
                        x_, x_r = xt[j % 3], ("xt", j % 3)
                        P.op("sp", lambda e: e.dma_start(out=x_[:], in_=xin[j * 128:(j + 1) * 128, :]), writes=[x_r], dma=True)
                        rsx[j] = rstd_of(x_[:], [x_r], D, D)

                    def S2(j, part):
                        own = j >= OWN0
                        x_, x_r = xt[j % 3], ("xt", j % 3)
                        h_, h_r = hTt[j % 2], ("hTt", j % 2)
                        pk, pk_r = ps_akv[j % 2], ("ps_akv", j % 2)
                        if part == 0:
                            norm_apply(x_[:], [x_r], rsx[j][0], rsx[j][1], h_[:], [h_r], ps_tr, "ps_tr")
                            return
                        for c in range(8):
                            P.op("pe", lambda e, c=c: e.matmul(pk[:, 0:384], lhsT=h_[:, c, :], rhs=w_a_kv[:, c, :], start=(c == 0), stop=(c == 7)),
                                 reads=[h_r, "w_a_kv"], writes=[pk_r])
                        if own:
                            for c in range(8):
                                P.op("pe", lambda e, c=c: e.matmul(ps_aq[:, 0:384], lhsT=h_[:, c, :], rhs=w_a_q[:, c, :], start=(c == 0), stop=(c == 7)),
                                     reads=[h_r, "w_a_q"], writes=["ps_aq"])
                        rskv[j] = rstd_of(pk[:, 0:256], [pk_r], 256, 256)
                        if own:
                            rsq[j] = rstd_of(ps_aq[:, 0:384], ["ps_aq"], 384, 384)

                    def S3(j, part):
                        own = j >= OWN0
                        pk, pk_r = ps_akv[j % 2], ("ps_akv", j % 2)
                        s_, s_r = akv_s[j % 2], ("akv_s", j % 2)
                        rs, rs_r = rskv[j]
                        t0, t1 = rtmp[0], rtmp[1]
                        if own:
                            t = j - OWN0
                            q_, q_r = aq_s[t % 2], ("aq_s", t % 2)
                        if part == 1:
                            if own:
                                S3b(j, t, t0, t1)
                            return
                        if part == 2:
                            if own:
                                S3c(j, t)
                            return
                        P.op("act", lambda e: e.activation(out=s_[:, 0:256], in_=pk[:, 0:256], func=AF.Copy, scale=rs),
                             reads=[pk_r, rs_r], writes=[s_r])
                        P.op("dve", lambda e: e.tensor_tensor(out=t0[:, 0:64], in0=pk[:, 256:320], in1=cos2[:, j, :], op=ALU.mult),
                             reads=[pk_r, "cos2"], writes=["rtmp0"])
                        P.op("dve", lambda e: e.tensor_tensor(out=t1[:, 0:64], in0=pk[:, 320:384], in1=ssgn[:, j, :], op=ALU.mult),
                             reads=[pk_r, "ssgn"], writes=["rtmp1"])
                        P.op("dve", lambda e: e.tensor_tensor(out=s_[:, 256:320], in0=t0[:, 0:64], in1=t1[:, 0:64], op=ALU.add),
                             reads=["rtmp0", "rtmp1"], writes=[s_r])
                        P.op("dve", lambda e: e.tensor_copy(out=s_[:, 448:512], in_=s_[:, 256:320]), reads=[s_r], writes=[s_r])
                        if own:
                            rq, rq_r = rsq[j]
                            P.op("act", lambda e: e.activation(out=q_[:], in_=ps_aq[:, 0:384], func=AF.Copy, scale=rq),
                                 reads=["ps_aq", rq_r], writes=[q_r])
                        for c in range(4):
                            P.op("pe", lambda e, c=c: e.transpose(out=ps_t2[:, c, :], in_=s_[:, c * 128:(c + 1) * 128], identity=identb[:]),
                                 reads=[s_r, "identb"], writes=["ps_t2"])
                        if own:
                            for c in range(3):
                                P.op("pe", lambda e, c=c: e.transpose(out=ps_t2[:, 4 + c, :], in_=q_[:, c * 128:(c + 1) * 128], identity=identb[:]),
                                     reads=[q_r, "identb"], writes=["ps_t2"])
                        P.op("dve", lambda e: e.tensor_tensor(out=ckvnT[:, :, j * 128:(j + 1) * 128], in0=ps_t2[:, 0:2, :], in1=gBkv[:], op=ALU.mult),
                             reads=["ps_t2", "gBkv"], writes=["ckvnT"])
                        P.op("dve", lambda e: e.tensor_copy(out=krT[:, :, j * 128:(j + 1) * 128], in_=ps_t2[:, 2:4, :]),
                             reads=["ps_t2"], writes=["krT"])
                        if not own:
                            return
                        P.op("dve", lambda e: e.tensor_tensor(out=cqnT[:, :, t * 128:(t + 1) * 128], in0=ps_t2[:, 4:7, :], in1=gBq[:], op=ALU.mult),
                             reads=["ps_t2", "gBq"], writes=[("cqnT", t)])

                    def S3b(j, t, t0, t1):
                        for c in range(3):
                            P.op("pe", lambda e, c=c: e.matmul(ps_qr[:], lhsT=cqnT[:, c, t * 128:(t + 1) * 128], rhs=w_uq_r[:, c, :],
                                                               start=(c == 0), stop=(c == 2)),
                                 reads=[("cqnT", t), "w_uq_r"], writes=["ps_qr"])
                        for c in range(3):
                            P.op("pe", lambda e, c=c: e.matmul(ps_qrs[:], lhsT=cqnT[:, c, t * 128:(t + 1) * 128], rhs=w_uq_rs[:, c, :],
                                                               start=(c == 0), stop=(c == 2)),
                                 reads=[("cqnT", t), "w_uq_rs"], writes=["ps_qrs"])
                        P.op("dve", lambda e: e.tensor_tensor(out=t0[:].rearrange("p (h r) -> p h r", h=8),
                                                              in0=ps_qr[:].rearrange("p (h r) -> p h r", h=8),
                                                              in1=cos2[:, j, :].unsqueeze(1).to_broadcast([128, 8, 64]), op=ALU.mult),
                             reads=["ps_qr", "cos2"], writes=["rtmp0"])
                        P.op("dve", lambda e: e.tensor_tensor(out=t1[:].rearrange("p (h r) -> p h r", h=8),
                                                              in0=ps_qrs[:].rearrange("p (h r) -> p h r", h=8),
                                                              in1=ssgn[:, j, :].unsqueeze(1).to_broadcast([128, 8, 64]), op=ALU.mult),
                             reads=["ps_qrs", "ssgn"], writes=["rtmp1"])
                        P.op("dve", lambda e: e.tensor_tensor(out=qrr[:], in0=t0[:], in1=t1[:], op=ALU.add),
                             reads=["rtmp0", "rtmp1"], writes=["qrr"])

                    def S3c(j, t):
                        for c in range(4):
                            P.op("pe", lambda e, c=c: e.transpose(out=ps_t3[:, c, :], in_=qrr[:, c * 128:(c + 1) * 128], identity=identb[:]),
                                 reads=["qrr", "identb"], writes=["ps_t3"])
                        P.op("act", lambda e: e.activation(out=qrT[:, :, t * 128:(t + 1) * 128], in_=ps_t3[:, 0:4, :], func=AF.Copy),
                             reads=["ps_t3"], writes=["qrT"])

                    for i in range(NCTX + 2):
                        has3 = i >= 2
                        has2 = 1 <= i <= NCTX
                        if has3:
                            S3(i - 2, 0)
                        if has2:
                            S2(i - 1, 0)
                        if has3:
                            S3(i - 2, 1)
                        if has2:
                            S2(i - 1, 1)
                        if has3:
                            S3(i - 2, 2)
                        if i < NCTX:
                            S1(i)
                    P.barrier()
                with ExitStack() as pb:
                    w_ukv_k = sbuf(pb, "w_ukv_k", [128, 2, 1024], BF16)
                    w_ukv_v = sbuf(pb, "w_ukv_v", [128, 2, 1024], BF16)
                    w_uq_n = sbuf(pb, "w_uq_n", [128, 3, 1024], BF16)
                    KT = [sbuf(pb, "KT", [128, NCTX * 128], BF16) for _ in range(2)]
                    Vh = [sbuf(pb, "Vh", [128, NCTX, 128], BF16) for _ in range(2)]
                    qn = [sbuf(pb, "qn", [128, TOWN], BF16) for _ in range(2)]
                    NPT = 4
                    PT = [sbuf(pb, "PT", [128, 512], BF16) for _ in range(NPT)]
                    rsum = [sbuf(pb, "rsum", [128, 512], F32) for _ in range(2)]
                    NPS = 3
                    ps_s = [psum(pb, "ps_s", [128, 512]) for _ in range(NPS)]
                    ps_o = [psum(pb, "ps_o", [128, 512]) for _ in range(2)]
                    ps_m = [psum(pb, "ps_m", [128, 512]) for _ in range(2)]
                    ps_gen = psum(pb, "ps_gen", [128, 512])
                    for dst, src, r in ((w_ukv_k, w_ukv_k_d, "w_ukv_k"), (w_ukv_v, w_ukv_v_d, "w_ukv_v"), (w_uq_n, w_uq_n_d, "w_uq_n")):
                        P.op("pool", lambda e, dst=dst, src=src: e.dma_start(out=dst[:], in_=src), writes=[r], dma=True)

                    for l_ in range(2):
                        for f in range(NF):
                            P.op("pool", lambda e, l_=l_, f=f: e.dma_start(out=wgu_bf[l_, f].rearrange("p a c j -> p a (c j)"),
                                                                            in_=wgu_d[l_, f].rearrange("p a c j -> p a (c j)")),
                                 writes=[("wgu_bf", l_, f)], dma=True)
                        for f2 in range(NF // 2):
                            P.op("pool", lambda e, l_=l_, f2=f2: e.dma_start(out=wd_bf[l_, :, 2 * f2:2 * f2 + 2, :],
                                                                              in_=wd_d[l_, :, 2 * f2:2 * f2 + 2, :]),
                                 writes=[("wd_bf", l_, f2)], dma=True)

                    def gen(h):
                        KT_, KT_r = KT[h % 2], ("KT", h % 2)
                        V_, V_r = Vh[h % 2], ("Vh", h % 2)
                        qn_, qn_r = qn[h % 2], ("qn", h % 2)
                        for g in range(NCTX // 4):
                            for c in range(2):
                                P.op("pe", lambda e, g=g, c=c: e.matmul(ps_gen[:], lhsT=w_ukv_k[:, c, h * 128:(h + 1) * 128],
                                                                        rhs=ckvnT[:, c, g * 512:(g + 1) * 512], start=(c == 0), stop=(c == 1)),
                                     reads=["w_ukv_k", "ckvnT"], writes=["ps_gen"])
                            P.op("dve", lambda e, g=g: e.tensor_copy(out=KT_[:, g * 512:(g + 1) * 512], in_=ps_gen[:]),
                                 reads=["ps_gen"], writes=[KT_r])
                            yield
                        for g in range(NCTX // 4):
                            for jj in range(4):
                                j = g * 4 + jj
                                for c in range(2):
                                    P.op("pe", lambda e, j=j, jj=jj, c=c: e.matmul(
                                        ps_gen[:, jj * 128:(jj + 1) * 128], lhsT=ckvnT[:, c, j * 128:(j + 1) * 128],
                                        rhs=w_ukv_v[:, c, h * 128:(h + 1) * 128], start=(c == 0), stop=(c == 1)),
                                        reads=["w_ukv_v", "ckvnT"], writes=["ps_gen"])
                            P.op("dve", lambda e, g=g: e.tensor_copy(out=V_[:, g * 4:(g + 1) * 4, :],
                                                                     in_=ps_gen[:].rearrange("p (a b) -> p a b", a=4)),
                                 reads=["ps_gen"], writes=[V_r])
                            yield
                        for c0 in range(0, TOWN, 512):
                            n = min(512, TOWN - c0)
                            for c in range(3):
                                P.op("pe", lambda e, c=c, c0=c0, n=n: e.matmul(ps_gen[:, 0:n], lhsT=w_uq_n[:, c, h * 128:(h + 1) * 128],
                                                                               rhs=cqnT[:, c, c0:c0 + n], start=(c == 0), stop=(c == 2)),
                                     reads=["w_uq_n"] + [("cqnT", t) for t in range(NOWN)], writes=["ps_gen"])
                            P.op("dve", lambda e, c0=c0, n=n: e.tensor_copy(out=qn_[:, c0:c0 + n], in_=ps_gen[:, 0:n]),
                                 reads=["ps_gen"], writes=[qn_r])
                            yield

                    steps = []
                    first_of_head = {}
                    gi = 0
                    for h in range(8):
                        first_of_head[len(steps)] = h
                        for sb0 in range(0, NOWN, 4):
                            ntile = min(4, NOWN - sb0)
                            nkb = OWN0 + sb0 + ntile
                            for kb in range(nkb):
                                steps.append(dict(h=h, sb0=sb0, W=ntile * 128, nkb=nkb, kb=kb, g=gi))
                            gi += 1

                    def emit_qk(i):
                        s_ = steps[i]
                        h, kb = s_["h"], s_["kb"]
                        idiag = kb - (OWN0 + s_["sb0"])
                        a0 = max(idiag, 0) * 128
                        n = s_["W"] - a0
                        q0 = s_["sb0"] * 128
                        ps_, ps_r = ps_s[i % NPS], ("ps_s", i % NPS)
                        KT_, qn_ = KT[h % 2], qn[h % 2]
                        P.op("pe", lambda e: e.matmul(ps_[:, 0:n], lhsT=KT_[:, kb * 128:(kb + 1) * 128],
                                                      rhs=qn_[:, q0 + a0:q0 + a0 + n], start=True, stop=False),
                             reads=[("KT", h % 2), ("qn", h % 2)], writes=[ps_r])
                        P.op("pe", lambda e: e.matmul(ps_[:, 0:n], lhsT=krT[:, h % 2, kb * 128:(kb + 1) * 128],
                                                      rhs=qrT[:, h // 2, q0 + a0:q0 + a0 + n], start=False, stop=True),
                             reads=["krT", "qrT"], writes=[ps_r])

                    def emit_rest(i):
                        s_ = steps[i]
                        h, kb, nkb, W, g = s_["h"], s_["kb"], s_["nkb"], s_["W"], s_["g"]
                        idiag = kb - (OWN0 + s_["sb0"])
                        a0 = max(idiag, 0) * 128
                        n = W - a0
                        q0 = s_["sb0"] * 128
                        ps_, ps_r = ps_s[i % NPS], ("ps_s", i % NPS)
                        pt_, pt_r = PT[i % NPT], ("PT", i % NPT)
                        po, po_r = ps_o[g % 2], ("ps_o", g % 2)
                        pm, pm_r = ps_m[g % 2], ("ps_m", g % 2)
                        V_ = Vh[h % 2]
                        P.op("act", lambda e: e.activation(out=pt_[:, 0:n], in_=ps_[:, 0:n], func=AF.Exp, scale=MLA_SCALE, bias=mbt[:, kb:kb + 1]),
                             reads=[ps_r, "mbt"], writes=[pt_r])
                        if idiag >= 0:
                            P.op("dve", lambda e: e.memset(pt_[64:128, 0:64], 0.0), writes=[pt_r])
                        P.op("pe", lambda e: e.matmul(po[:, a0:a0 + n], lhsT=V_[:, kb, :], rhs=pt_[:, 0:n], start=(kb == 0), stop=(kb == nkb - 1)),
                             reads=[("Vh", h % 2), pt_r], writes=[po_r])
                        P.op("pe", lambda e: e.matmul(pm[:, a0:a0 + n], lhsT=onesb[:], rhs=pt_[:, 0:n], start=(kb == 0), stop=(kb == nkb - 1)),
                             reads=["onesb", pt_r], writes=[pm_r])
                        if kb == nkb - 1:
                            rs_, rs_r = rsum[g % 2], ("rsum", g % 2)
                            P.op("dve", lambda e: e.reciprocal(out=rs_[:, 0:W], in_=pm[:, 0:W]), reads=[pm_r], writes=[rs_r])
                            P.op("dve", lambda e: e.tensor_tensor(out=oT[:, h, q0:q0 + W], in0=po[:, 0:W], in1=rs_[:, 0:W], op=ALU.mult),
                                 reads=[po_r, rs_r], writes=["oT"])

                    LOOK = 2
                    for _ in gen(0):
                        pass
                    pending_gen = {}
                    for i0, h in first_of_head.items():
                        if h + 1 < 8:
                            pending_gen[i0 + 8] = h + 1
                    cur_gen = None
                    for i in range(len(steps) + LOOK):
                        if i < len(steps):
                            emit_qk(i)
                        if i in pending_gen:
                            cur_gen = gen(pending_gen[i])
                        if cur_gen is not None and i % 4 == 0:
                            if next(cur_gen, "done") == "done":
                                cur_gen = None
                        if i >= LOOK:
                            emit_rest(i - LOOK)
                    assert cur_gen is None
                    P.barrier()
            x1 = sbuf(top, "x1", [128, NOWN, D], F32)
            with ExitStack() as pc:
                w_o = sbuf(pc, "w_o", [128, 8, D], BF16, side="right")
                xr = [sbuf(pc, "xr", [128, D], F32) for _ in range(2)]
                ps_y = [psum(pc, "ps_y", [128, D]) for _ in range(2)]
                P.op("pool", lambda e: e.dma_start(out=w_o[:], in_=w_o_d), writes=["w_o"], dma=True)
                load_gT(0)
                for t in range(NOWN):
                    xr_, xr_r = xr[t % 2], ("xr", t % 2)
                    py, py_r = ps_y[t % 2], ("ps_y", t % 2)
                    P.op("sp", lambda e, xr_=xr_, t=t: e.dma_start(out=xr_[:], in_=xin[(OWN0 + t) * 128:(OWN0 + t + 1) * 128, :]),
                         writes=[xr_r], dma=True)
                    for hf in range(2):
                        for h in range(8):
                            P.op("pe", lambda e, py=py, t=t, hf=hf, h=h: e.matmul(
                                py[:, hf * 512:(hf + 1) * 512], lhsT=oT[:, h, t * 128:(t + 1) * 128],
                                rhs=w_o[:, h, hf * 512:(hf + 1) * 512], start=(h == 0), stop=(h == 7)),
                                reads=["oT", "w_o"], writes=[py_r])
                    post_norm_residual(py[:], py_r, x1[:, t, :], [("x1", t)], xr_[:], [xr_r])
                P.barrier()

        def dump_and_finish():
            for t in range(NOWN):
                P.op("sp", lambda e, t=t: e.dma_start(out=out_d[t * 128:(t + 1) * 128, :], in_=x1[:, t, :]),
                     reads=[("x1", t)], writes=[("out", t)], dma=True)

        ps_tr_top = None
        if stop == "C":
            dump_and_finish()
        else:
            with ExitStack() as rest:
                ps_tr_top = psum(rest, "ps_trt", [128, 8, 128], BF16)
                ffn(rest, x1, 0, C_GPRE + 8, 1, ps_tr_top, final_out=False)
                if stop == "D":
                    dump_and_finish()
                else:
                    with ExitStack() as pf:
                        w_kk = sbuf(pf, "w_kk", [128, 8, 1024], BF16)
                        w_kv_ = sbuf(pf, "w_kvv", [128, 8, 1024], BF16)
                        w_q2 = sbuf(pf, "w_q2", [128, 8, 1024], BF16)
                        w_o2 = sbuf(pf, "w_o2", [128, 8, 1024], BF16)
                        NR = 6
                        K2T = sbuf(pf, "K2T", [128, 8, NR * 128], BF16)
                        V2 = sbuf(pf, "V2", [128, NR, 1024], BF16)
                        hb = sbuf(pf, "hb", [128, 8, 256], BF16)
                        oT2 = sbuf(pf, "oT2", [128, 8, 256], BF16)
                        QP = sbuf(pf, "QP", [128, 8, 2, 2, 128], BF16)
                        NPT = 4
                        PT = [sbuf(pf, "PT2", [128, 512], BF16) for _ in range(NPT)]
                        Bt = [sbuf(pf, "Bt", [128, 5, 2, 128], BF16) for _ in range(2)]
                        rsum = sbuf(pf, "rsum2", [128, 512], F32)
                        rstdF = sbuf(pf, "rstdF", [128, NOWN], F32)
                        ps_gy = psum(pf, "ps_gy", [128, D])
                        ps_s2 = psum(pf, "ps_s2", [128, 512])
                        ps_om = psum(pf, "ps_om", [128, 2048])
                        ps_sr = [ps_gy[:, 0:512], ps_gy[:, 512:1024], ps_s2[:]]
                        GY = [("ps_s2", 0), ("ps_s2", 1)]
                        ps_o = [ps_om[:, 0:512], ps_om[:, 512:1024]]
                        ps_m = [ps_om[:, 1024:1536], ps_om[:, 1536:2048]]
                        ybuf = [(ps_gy[:], GY), (ps_om[:, 0:1024], [("ps_o2", 0), ("ps_o2", 1)])]
                        for dst, src, r in ((w_kk, w_kvs_k_d, "w_kk"), (w_kv_, w_kvs_v_d, "w_kvv"), (w_q2, w_q2_d, "w_q2"), (w_o2, w_o2_d, "w_o2")):
                            P.op("pool", lambda e, dst=dst, src=src: e.dma_start(out=dst[:], in_=src), writes=[r], dma=True)
                        for w_, r, col in ((w_kk, "w_kk", C_GPRE + 16), (w_kv_, "w_kvv", C_GPRE + 16), (w_q2, "w_q2", C_GPRE + 24)):
                            for c in range(8):
                                P.op("dve", lambda e, w_=w_, c=c, col=col: e.tensor_scalar(out=w_[:, c, :], in0=w_[:, c, :],
                                                                                           scalar1=consts[:, col + c:col + c + 1], scalar2=None, op0=ALU.mult),
                                     reads=[r, "consts"], writes=[r])
                        load_gT(2)
                        P.op("dve", lambda e: e.memset(QP[:], 0.0), writes=["QP"])
                        for t in range(NOWN):
                            rs, rs_r = rstd_of(x1[:, t, :], [("x1", t)], D, D)
                            P.op("dve", lambda e, t=t, rs=rs: e.tensor_copy(out=rstdF[:, t:t + 1], in_=rs), reads=[rs_r], writes=[("rstdF", t)])
                        gstep = [0]
                        gpair = [0]
                        yk = [0]

                        def stage_a(sb0):
                            for tt in range(2):
                                t = sb0 + tt
                                xs = xs_ring[xs_i[0] % 2]
                                xs_r = ("xs", xs_i[0] % 2)
                                xs_i[0] += 1
                                P.op("act", lambda e, xs=xs, t=t: e.activation(out=xs[:], in_=x1[:, t, :], func=AF.Copy, scale=rstdF[:, t:t + 1]),
                                     reads=[("x1", t), ("rstdF", t)], writes=[xs_r])
                                for c in range(8):
                                    P.op("pe", lambda e, xs=xs, c=c: e.transpose(out=ps_tr_top[:, c, :], in_=xs[:, c * 128:(c + 1) * 128], identity=identb[:]),
                                         reads=[xs_r, "identb"], writes=["ps_tr"])
                                P.op("dve", lambda e, tt=tt: e.tensor_copy(out=hb[:, :, tt * 128:(tt + 1) * 128], in_=ps_tr_top[:]),
                                     reads=["ps_tr"], writes=["hb"])

                        def stage_b(sb0):
                            slot0 = sb0 % NR
                            for pr in range(8):
                                for c in range(8):
                                    P.op("pe", lambda e, pr=pr, c=c: e.matmul(ps_gy[:, 0:256], lhsT=w_kk[:, c, pr * 128:(pr + 1) * 128],
                                                                              rhs=hb[:, c, :], start=(c == 0), stop=(c == 7)),
                                         reads=["w_kk", "hb"], writes=[GY[0]])
                                P.op("dve", lambda e, pr=pr: e.tensor_copy(out=K2T[:, pr, slot0 * 128:(slot0 + 2) * 128], in_=ps_gy[:, 0:256]),
                                     reads=[GY[0]], writes=["K2T"])
                                for c in range(8):
                                    P.op("pe", lambda e, pr=pr, c=c: e.matmul(ps_gy[:, 512:768], lhsT=w_q2[:, c, pr * 128:(pr + 1) * 128],
                                                                              rhs=hb[:, c, :], start=(c == 0), stop=(c == 7)),
                                         reads=["w_q2", "hb"], writes=[GY[1]])
                                for hh in range(2):
                                    P.op("act", lambda e, pr=pr, hh=hh: e.activation(
                                        out=QP[hh * 64:(hh + 1) * 64, pr, :, hh, :],
                                        in_=ps_gy[hh * 64:(hh + 1) * 64, 512:768].rearrange("p (t q) -> p t q", q=128), func=AF.Copy, scale=0.125),
                                        reads=[GY[1]], writes=["QP"])
                            for tt in range(2):
                                for hf in range(2):
                                    for c in range(8):
                                        P.op("pe", lambda e, tt=tt, hf=hf, c=c: e.matmul(
                                            ps_gy[:, hf * 512:(hf + 1) * 512], lhsT=hb[:, c, tt * 128:(tt + 1) * 128],
                                            rhs=w_kv_[:, c, hf * 512:(hf + 1) * 512], start=(c == 0), stop=(c == 7)),
                                            reads=["w_kvv", "hb"], writes=[GY[hf]])
                                P.op("dve", lambda e, tt=tt: e.tensor_copy(out=V2[:, slot0 + tt, :], in_=ps_gy[:]), reads=GY, writes=["V2"])

                        def stage_c(sb0):
                            kb_lo = max(0, sb0 - 4)
                            kbs = list(range(kb_lo, sb0 + 2))
                            steps = [(pr, ki_) for pr in range(8) for ki_ in range(len(kbs))]

                            def geo(kb):
                                js = [j for j in (sb0, sb0 + 1) if 0 <= j - kb <= 4]
                                return (js[0] - sb0), len(js), js[0] - kb

                            def qk(i, gs):
                                pr, ki_ = steps[i]
                                kb = kbs[ki_]
                                t0_, nt, d0 = geo(kb)
                                slot = kb % NR
                                ps_, ps_r = ps_sr[gs % 3], ("ps_s2", gs % 3)
                                b_, b_r = Bt[(gpair[0] + pr) % 2], ("Bt", (gpair[0] + pr) % 2)
                                if ki_ == 0:
                                    P.op("pool", lambda e: e.dma_start(out=b_[:], in_=btab_d[pr]), writes=[b_r], dma=True)
                                ncol = nt * 256
                                P.op("pe", lambda e: e.matmul(ps_[:, 0:ncol], lhsT=K2T[:, pr, slot * 128:(slot + 1) * 128],
                                                              rhs=QP[:, pr, t0_:t0_ + nt, :, :], start=True, stop=False),
                                     reads=["K2T", "QP"], writes=[ps_r])
                                P.op("pe", lambda e: e.matmul(ps_[:, 0:ncol], lhsT=identb[:], rhs=b_[:, d0:d0 + nt, :, :], start=False, stop=True),
                                     reads=["identb", b_r], writes=[ps_r])

                            def rest(i, gs):
                                pr, ki_ = steps[i]
                                kb = kbs[ki_]
                                t0_, nt, d0 = geo(kb)
                                slot = kb % NR
                                ncol = nt * 256
                                ps_, ps_r = ps_sr[gs % 3], ("ps_s2", gs % 3)
                                pt_, pt_r = PT[gs % NPT], ("PT2", gs % NPT)
                                gp = gpair[0] + pr
                                po, po_r = ps_o[gp % 2], ("ps_o2", gp % 2)
                                pm, pm_r = ps_m[gp % 2], ("ps_m2", gp % 2)
                                first, last = (ki_ == 0), (ki_ == len(kbs) - 1)
                                P.op("act", lambda e: e.activation(out=pt_[:, 0:ncol], in_=ps_[:, 0:ncol], func=AF.Exp), reads=[ps_r], writes=[pt_r])
                                ptv = pt_[:, 0:ncol].rearrange("p (t h q) -> p t h q", h=2, q=128)
                                for hh in range(2):
                                    h = 2 * pr + hh
                                    P.op("pe", lambda e, hh=hh, h=h: e.matmul(
                                        po[hh * 64:(hh + 1) * 64, t0_ * 128:(t0_ + nt) * 128].rearrange("p (t q) -> p t q", q=128),
                                        lhsT=V2[:, slot, h * 64:(h + 1) * 64], rhs=ptv[:, :, hh, :], start=first, stop=last, skip_group_check=True),
                                        reads=["V2", pt_r], writes=[po_r])
                                P.op("pe", lambda e: e.matmul(pm[:, t0_ * 256:t0_ * 256 + ncol], lhsT=onesb[:], rhs=pt_[:, 0:ncol], start=first, stop=last),
                                     reads=["onesb", pt_r], writes=[pm_r])
                                if last:
                                    P.op("dve", lambda e: e.reciprocal(out=rsum[:], in_=pm), reads=[pm_r], writes=["rsum2"])
                                    rsv = rsum[:].rearrange("p (t h q) -> p t h q", h=2, q=128)
                                    for hh in range(2):
                                        P.op("dve", lambda e, hh=hh: e.tensor_tensor(
                                            out=oT2[hh * 64:(hh + 1) * 64, pr, :].rearrange("p (t q) -> p t q", q=128),
                                            in0=po[hh * 64:(hh + 1) * 64, 0:256].rearrange("p (t q) -> p t q", q=128),
                                            in1=rsv[hh * 64:(hh + 1) * 64, :, hh, :], op=ALU.mult),
                                            reads=[po_r, "rsum2"], writes=["oT2"])

                            LOOK = 2
                            for i in range(len(steps) + LOOK):
                                if i < len(steps):
                                    qk(i, gstep[0] + i)
                                if i >= LOOK:
                                    rest(i - LOOK, gstep[0] + i - LOOK)
                            gstep[0] += len(steps)
                            gpair[0] += 8

                        def stage_d(sb0, tt):
                            py, py_r = ybuf[yk[0] % 2]
                            yk[0] += 1
                            for hf in range(2):
                                for pr in range(8):
                                    P.op("pe", lambda e, py=py, tt=tt, hf=hf, pr=pr: e.matmul(
                                        py[:, hf * 512:(hf + 1) * 512], lhsT=oT2[:, pr, tt * 128:(tt + 1) * 128],
                                        rhs=w_o2[:, pr, hf * 512:(hf + 1) * 512], start=(pr == 0), stop=(pr == 7)),
                                        reads=["oT2", "w_o2"], writes=py_r)
                            t = sb0 + tt
                            post_norm_residual(py, py_r, x1[:, t, :], [("x1", t)], x1[:, t, :], [("x1", t)])

                        stage_a(0)
                        stage_b(0)
                        for sb0 in range(0, NOWN, 2):
                            stage_c(sb0)
                            nxt = sb0 + 2 < NOWN
                            if nxt:
                                stage_a(sb0 + 2)
                            stage_d(sb0, 0)
                            stage_d(sb0, 1)
                            if nxt:
                                stage_b(sb0 + 2)
                        P.barrier()
                    if stop == "F":
                        dump_and_finish()
                    else:
                        ffn(rest, x1, 1, C_GPRE + 32, 3, ps_tr_top, final_out=True)
        P.op("sp", lambda e: None, reads=[("out", t) for t in range(NOWN)])
        P.emit(top)
    return nc


def _kc(w):
    k, n = w.shape
    return np.ascontiguousarray(w.reshape(k // 128, 128, n).transpose(1, 0, 2))


def _prep(inputs):
    f = lambda a: np.asarray(a, dtype=np.float32)
    x = f(inputs["x"])
    positions = np.asarray(inputs["positions"]).astype(np.int32)
    w_a = f(inputs["mla_w_a"])[0]
    w_uq = f(inputs["mla_w_uq"])[0]
    w_ukv = f(inputs["mla_w_ukv"])[0]
    sw = (np.arange(64) + 32) % 64
    shared = {}
    shared["w_a_kv"] = _kc(np.concatenate([w_a[:, 384:640], w_a[:, 640:704], w_a[:, 640:704][:, sw]], axis=1))
    shared["w_a_q"] = _kc(w_a[:, 0:384])
    uq = w_uq.reshape(384, 8, 192)
    shared["w_uq_n"] = _kc(np.ascontiguousarray(uq[:, :, 0:128]).reshape(384, 1024))
    shared["w_uq_r"] = _kc(np.ascontiguousarray(uq[:, :, 128:192]).reshape(384, 512))
    shared["w_uq_rs"] = _kc(np.ascontiguousarray(uq[:, :, 128:192][:, :, sw]).reshape(384, 512))
    ukv = w_ukv.reshape(256, 8, 256)
    shared["w_ukv_k"] = _kc(np.ascontiguousarray(ukv[:, :, 0:128]).reshape(256, 1024))
    shared["w_ukv_v"] = _kc(np.ascontiguousarray(ukv[:, :, 128:256]).reshape(256, 1024))
    shared["w_o"] = _kc(f(inputs["mla_w_o"])[0])
    wg = f(inputs["ffn_w_gate"]).reshape(2, 8, 128, NF, 128)
    wu = f(inputs["ffn_w_up"]).reshape(2, 8, 128, NF, 128)
    wgu = np.stack([wg, wu], axis=1)
    shared["wgu"] = np.ascontiguousarray(wgu.transpose(0, 4, 3, 1, 2, 5))
    shared["wd"] = np.ascontiguousarray(f(inputs["ffn_w_down"]).reshape(2, NF, 128, D).transpose(0, 2, 1, 3))
    wkv = f(inputs["w_kv_shared"])
    shared["w_kvs_k"] = _kc(wkv[:, 0:1024])
    shared["w_kvs_v"] = _kc(wkv[:, 1024:2048])
    shared["w_q2"] = _kc(f(inputs["b_w_q"])[0])
    shared["w_o2"] = _kc(f(inputs["b_w_o"])[0])
    rel = f(inputs["b_rel_table"])[0]
    kl = np.arange(128)[:, None]
    cc = np.arange(640)[None, :]
    idx = np.clip(cc - kl, -63, 256) + 63
    bt = rel[:, idx]
    dl = cc // 128
    qh = (cc % 128) >= 64
    kh = kl >= 64
    invalid = ((dl == 0) & (~qh) & kh) | ((dl == 4) & qh & (~kh))
    bt = np.where(invalid[None], np.float32(NEG), bt).astype(np.float32)
    shared["btab"] = np.ascontiguousarray(bt.reshape(8, 2, 128, 5, 128).transpose(0, 2, 3, 1, 4))
    consts = np.zeros((128, NCONST), np.float32)
    pre = [f(inputs["attn_pre_g"])[0], f(inputs["ffn_pre_g"])[0], f(inputs["kv_src_g"]),
           f(inputs["attn_pre_g"])[1], f(inputs["ffn_pre_g"])[1]]
    for i, g in enumerate(pre):
        consts[:, C_GPRE + 8 * i:C_GPRE + 8 * (i + 1)] = g.reshape(8, 128).T
    consts[:, C_GQ:C_GQ + 3] = f(inputs["mla_g_q"])[0].reshape(3, 128).T
    consts[:, C_GKV:C_GKV + 2] = f(inputs["mla_g_kv"])[0].reshape(2, 128).T
    inv = (1.0 / (np.float32(10000.0) ** (np.arange(0, 64, 2, dtype=np.float32) / np.float32(64)))).astype(np.float32)
    consts[:, C_INV:C_INV + 32] = (inv.astype(np.float64) / (2 * np.pi)).astype(np.float32)[None, :]
    consts[:, C_IDENT:C_IDENT + 128] = np.eye(128, dtype=np.float32)
    shared["consts"] = consts
    shared["gpost"] = np.stack([f(inputs["attn_post_g"])[0], f(inputs["ffn_post_g"])[0],
                                f(inputs["attn_post_g"])[1], f(inputs["ffn_post_g"])[1]]).astype(np.float32)
    in_maps = []
    for core in range(8):
        b, half = core // 2, core % 2
        m = dict(shared)
        if half == 0:
            xin = np.concatenate([np.zeros((OWN0 * 128, D), np.float32), x[b, 0:TOWN]], axis=0)
            pos = np.concatenate([np.zeros(OWN0 * 128, np.int32), positions[b, 0:TOWN]])
            mb = np.zeros((128, NCTX), np.float32)
            mb[:, 0:OWN0] = NEG
        else:
            xin = x[b]
            pos = positions[b]
            mb = np.zeros((128, NCTX), np.float32)
        m["xin"] = np.ascontiguousarray(xin)
        m["posT"] = np.ascontiguousarray(pos.reshape(NCTX, 128).T.astype(np.int32))
        m["mb"] = mb
        in_maps.append(m)
    return in_maps


_NC_CACHE = {}


def _run(inputs, stop="G"):
    if stop not in _NC_CACHE:
        _NC_CACHE[stop] = build(stop)
    nc = _NC_CACHE[stop]
    in_maps = _prep(inputs)
    res = run_bass_kernel_spmd(nc, in_maps, core_ids=list(range(8)))
    out = np.zeros((4, 4096, D), np.float32)
    for core in range(8):
        b, half = core // 2, core % 2
        o = np.asarray(res.results[core]["out"], dtype=np.float32)
        if half == 0:
            out[b, 0:TOWN] = o
        else:
            out[b, TOWN:4096] = o[TOWN - (4096 - TOWN):]
    return out


def kernel(**inputs):
    return _run(inputs, "G")
```
